# Optimizing a Trainium2 kernel written in Bass

```python
import functools
import jax, jax.numpy as jnp
from jax import lax
import numpy as np

D_MODEL = 1024
BATCH = 4
SEQ = 4096
DEPTH = 1
DEC_BATCH = 32
DEC_SEQ = 64
PAST_LEN = 1024

CHUNK = 64
N_HEADS = 16
N_KV_HEADS = 4
HEAD_DIM = 64
GROUP = N_HEADS // N_KV_HEADS
WINDOW = 128
N_BACK = WINDOW // CHUNK
Q_W = N_HEADS * HEAD_DIM
KV_W = N_KV_HEADS * HEAD_DIM
D_RNN = 1280
N_RNN_BLOCKS = 10
RNN_BLOCK = D_RNN // N_RNN_BLOCKS
CONV_W = 4
LRU_C = 8.0
D_FF = ((8 * D_MODEL // 3 + 255) // 256) * 256
IN_SIZES = (Q_W, KV_W, KV_W, D_RNN, D_RNN, D_MODEL, D_MODEL)
IN_COLS = Q_W + 2 * KV_W + 2 * D_RNN + 2 * D_MODEL
EPS = 1e-6
NEG_INF = -1e30

kernel_name = 'hybrid_swa_sink_rglru_stream_step'


def _rmsnorm(x, g):
    x32 = x.astype(jnp.float32)
    y = x32 * lax.rsqrt(jnp.mean(x32 * x32, axis=-1, keepdims=True) + EPS) * g.astype(jnp.float32)
    return y.astype(x.dtype)


def _alibi_slopes():
    h = jnp.arange(1, N_HEADS + 1, dtype=jnp.float32)
    return jnp.exp2(-8.0 * h / N_HEADS).reshape(N_KV_HEADS, GROUP)


def _sink_attention(q, k, v, qpos, kpos, sinks):
    s = jnp.einsum('bnqhgd,bnshd->bnhgqs', q, k,
                   preferred_element_type=jnp.float32) * (HEAD_DIM ** -0.5)
    dist = jnp.abs(qpos[:, :, None] - kpos[:, None, :]).astype(jnp.float32)
    s = s - _alibi_slopes()[None, None, :, :, None, None] * dist[None, :, None, None]
    qc = (qpos // CHUNK)[:, :, None]
    kc = (kpos // CHUNK)[:, None, :]
    valid = (kpos[:, None, :] >= 0) & (kc <= qc) & (kc >= qc - N_BACK)
    s = jnp.where(valid[None, :, None, None], s, NEG_INF)
    sink = jnp.broadcast_to(sinks.astype(jnp.float32).reshape(1, 1, N_KV_HEADS, GROUP, 1, 1),
                            s.shape[:-1] + (1,))
    p = jax.nn.softmax(jnp.concatenate([s, sink], axis=-1), axis=-1)[..., :-1]
    return jnp.einsum('bnhgqs,bnshd->bnqhgd', p.astype(v.dtype), v)


def _attn_prompt(q, k, v, sinks):
    B, S = q.shape[:2]
    nc = S // CHUNK
    kp = jnp.pad(k, ((0, 0), (WINDOW, 0), (0, 0), (0, 0)))
    vp = jnp.pad(v, ((0, 0), (WINDOW, 0), (0, 0), (0, 0)))
    kch = kp.reshape(B, nc + N_BACK, CHUNK, N_KV_HEADS, HEAD_DIM)
    vch = vp.reshape(B, nc + N_BACK, CHUNK, N_KV_HEADS, HEAD_DIM)
    kb = jnp.concatenate([kch[:, j:j + nc] for j in range(N_BACK + 1)], axis=2)
    vb = jnp.concatenate([vch[:, j:j + nc] for j in range(N_BACK + 1)], axis=2)
    qb = q.reshape(B, nc, CHUNK, N_KV_HEADS, GROUP, HEAD_DIM)
    qpos = jnp.arange(nc)[:, None] * CHUNK + jnp.arange(CHUNK)[None, :]
    kpos = (jnp.arange(nc)[:, None] - N_BACK) * CHUNK + jnp.arange(WINDOW + CHUNK)[None, :]
    o = _sink_attention(qb, kb, vb, qpos, kpos, sinks).reshape(B, S, Q_W)
    return o, kp[:, -WINDOW:], vp[:, -WINDOW:]


def _attn_sample(q, k, v, sinks, k_cache, v_cache):
    B, L = q.shape[:2]
    kk = jnp.concatenate([k_cache.astype(k.dtype), k], axis=1)
    vv = jnp.concatenate([v_cache.astype(v.dtype), v], axis=1)
    qpos = (PAST_LEN + jnp.arange(L))[None, :]
    kpos = (PAST_LEN - WINDOW + jnp.arange(WINDOW + L))[None, :]
    o = _sink_attention(q[:, None], kk[:, None], vv[:, None], qpos, kpos, sinks).reshape(B, L, Q_W)
    return o, kk[:, -WINDOW:], vv[:, -WINDOW:]


def _causal_conv(xr, buf, w, b):
    L = xr.shape[1]
    xp = jnp.concatenate([buf.astype(xr.dtype), xr], axis=1)
    out = b + sum(xp[:, j:j + L] * w[j] for j in range(CONV_W))
    return out, xp[:, -(CONV_W - 1):]


def _rglru(xr, h0, w_a, b_a, w_x, b_x, lam):
    B, L, _ = xr.shape
    x32 = xr.astype(jnp.float32)
    xb = x32.reshape(B, L, N_RNN_BLOCKS, RNN_BLOCK)
    r = jax.nn.sigmoid(jnp.einsum('blnc,ncd->blnd', xb, w_a.astype(jnp.float32)).reshape(B, L, D_RNN)
                       + b_a.astype(jnp.float32))
    i = jax.nn.sigmoid(jnp.einsum('blnc,ncd->blnd', xb, w_x.astype(jnp.float32)).reshape(B, L, D_RNN)
                       + b_x.astype(jnp.float32))
    log_a = -LRU_C * r * jax.nn.softplus(-lam.astype(jnp.float32))
    a = jnp.exp(log_a)
    bterm = jnp.sqrt(-jnp.expm1(2.0 * log_a)) * (i * x32)
    bterm = bterm.at[:, 0].add(a[:, 0] * h0.astype(jnp.float32))

    def comb(left, right):
        a1, b1 = left
        a2, b2 = right
        return a1 * a2, a2 * b1 + b2

    _, h = lax.associative_scan(comb, (a, bterm), axis=1)
    return h, h[:, -1]


def _layer(x, c, attn_fn, conv_buf, h0, w_ada, b_ada, g_pre_mix, g_post_mix, w_in, attn_sinks,
           w_conv, b_conv, w_rg_a, b_rg_a, w_rg_x, b_rg_x, rg_lambda, w_attn_o, w_rnn_o, w_out,
           g_pre_ffn, g_post_ffn, w_ffn_gate, w_ffn_up, w_ffn_down):
    B, L, _ = x.shape
    mod = (jax.nn.silu(c) @ w_ada + b_ada).reshape(B, 6, D_MODEL)[:, :, None, :]
    sh_m, sc_m, gt_m, sh_f, sc_f, gt_f = (mod[:, j] for j in range(6))

    u = _rmsnorm(x, g_pre_mix) * (1.0 + sc_m) + sh_m
    z = u @ w_in
    parts = []
    off = 0
    for n in IN_SIZES:
        parts.append(z[..., off:off + n])
        off += n
    q, k, v, xr, yr, ga, gr = parts
    q = q.reshape(B, L, N_KV_HEADS, GROUP, HEAD_DIM)
    k = k.reshape(B, L, N_KV_HEADS, HEAD_DIM)
    v = v.reshape(B, L, N_KV_HEADS, HEAD_DIM)
    attn_o, k_state, v_state = attn_fn(q, k, v, attn_sinks)
    xc, conv_state = _causal_conv(xr, conv_buf, w_conv, b_conv)
    h, h_last = _rglru(xc, h0, w_rg_a, b_rg_a, w_rg_x, b_rg_x, rg_lambda)
    rnn_o = h.astype(x.dtype) * jax.nn.gelu(yr)
    merged = jax.nn.sigmoid(ga) * (attn_o @ w_attn_o) + jax.nn.sigmoid(gr) * (rnn_o @ w_rnn_o)
    x = x + gt_m * _rmsnorm(merged @ w_out, g_post_mix)

    u = _rmsnorm(x, g_pre_ffn) * (1.0 + sc_f) + sh_f
    f = (jax.nn.silu(u @ w_ffn_gate) * (u @ w_ffn_up)) @ w_ffn_down
    x = x + gt_f * _rmsnorm(f, g_post_ffn)
    return x, k_state, v_state, conv_state, h_last


def setup_inputs(seed: int = 0) -> dict:
    key = jax.random.key(seed)
    ks = iter(jax.random.split(key, 40))

    def nrm(shape, scale):
        return jax.random.normal(next(ks), shape, jnp.float32) * scale

    def gain(shape):
        return 1.0 + 0.01 * jax.random.normal(next(ks), shape, jnp.float32)

    u = jax.random.uniform(next(ks), (DEPTH, D_RNN), jnp.float32, minval=0.9, maxval=0.999)
    return {
        'x_prompt': nrm((BATCH, SEQ, D_MODEL), 1.0),
        'x_sample': nrm((DEC_BATCH, DEC_SEQ, D_MODEL), 1.0),
        'c_prompt': nrm((BATCH, D_MODEL), 1.0),
        'c_sample': nrm((DEC_BATCH, D_MODEL), 1.0),
        'cache_k': nrm((DEPTH, DEC_BATCH, WINDOW, N_KV_HEADS, HEAD_DIM), 1.0),
        'cache_v': nrm((DEPTH, DEC_BATCH, WINDOW, N_KV_HEADS, HEAD_DIM), 1.0),
        'state_conv': nrm((DEPTH, DEC_BATCH, CONV_W - 1, D_RNN), 1.0),
        'state_h': nrm((DEPTH, DEC_BATCH, D_RNN), 0.5),
        'w_ada': nrm((DEPTH, D_MODEL, 6 * D_MODEL), 0.5 * D_MODEL ** -0.5),
        'b_ada': nrm((DEPTH, 6 * D_MODEL), 0.01),
        'g_pre_mix': gain((DEPTH, D_MODEL)),
        'g_post_mix': gain((DEPTH, D_MODEL)),
        'w_in': nrm((DEPTH, D_MODEL, IN_COLS), D_MODEL ** -0.5),
        'attn_sinks': nrm((DEPTH, N_HEADS), 1.0),
        'w_conv': nrm((DEPTH, CONV_W, D_RNN), CONV_W ** -0.5),
        'b_conv': nrm((DEPTH, D_RNN), 0.01),
        'w_rg_a': nrm((DEPTH, N_RNN_BLOCKS, RNN_BLOCK, RNN_BLOCK), RNN_BLOCK ** -0.5),
        'b_rg_a': nrm((DEPTH, D_RNN), 0.01),
        'w_rg_x': nrm((DEPTH, N_RNN_BLOCKS, RNN_BLOCK, RNN_BLOCK), RNN_BLOCK ** -0.5),
        'b_rg_x': nrm((DEPTH, D_RNN), 0.01),
        'rg_lambda': jnp.log(u) - jnp.log1p(-u),
        'w_attn_o': nrm((DEPTH, Q_W, D_MODEL), Q_W ** -0.5),
        'w_rnn_o': nrm((DEPTH, D_RNN, D_MODEL), D_RNN ** -0.5),
        'w_out': nrm((DEPTH, D_MODEL, D_MODEL), D_MODEL ** -0.5),
        'g_pre_ffn': gain((DEPTH, D_MODEL)),
        'g_post_ffn': gain((DEPTH, D_MODEL)),
        'w_ffn_gate': nrm((DEPTH, D_MODEL, D_FF), D_MODEL ** -0.5),
        'w_ffn_up': nrm((DEPTH, D_MODEL, D_FF), D_MODEL ** -0.5),
        'w_ffn_down': nrm((DEPTH, D_FF, D_MODEL), D_FF ** -0.5),
    }


def reference(x_prompt, x_sample, c_prompt, c_sample, cache_k, cache_v, state_conv, state_h,
              w_ada, b_ada, g_pre_mix, g_post_mix, w_in, attn_sinks, w_conv, b_conv,
              w_rg_a, b_rg_a, w_rg_x, b_rg_x, rg_lambda, w_attn_o, w_rnn_o, w_out,
              g_pre_ffn, g_post_ffn, w_ffn_gate, w_ffn_up, w_ffn_down):
    layer_params = (w_ada, b_ada, g_pre_mix, g_post_mix, w_in, attn_sinks, w_conv, b_conv,
                    w_rg_a, b_rg_a, w_rg_x, b_rg_x, rg_lambda, w_attn_o, w_rnn_o, w_out,
                    g_pre_ffn, g_post_ffn, w_ffn_gate, w_ffn_up, w_ffn_down)
    xp, xs = x_prompt, x_sample
    kps, vps, cps, hps, kss, vss, css, hss = [], [], [], [], [], [], [], []
    for l in range(DEPTH):
        lp = [w[l] for w in layer_params]
        conv0 = jnp.zeros((xp.shape[0], CONV_W - 1, D_RNN), xp.dtype)
        h0 = jnp.zeros((xp.shape[0], D_RNN), jnp.float32)
        xp, kp_, vp_, cp_, hp_ = _layer(xp, c_prompt, _attn_prompt, conv0, h0, *lp)
        attn_s = functools.partial(_attn_sample, k_cache=cache_k[l], v_cache=cache_v[l])
        xs, ks_, vs_, cs_, hs_ = _layer(xs, c_sample, attn_s, state_conv[l], state_h[l], *lp)
        kps.append(kp_); vps.append(vp_); cps.append(cp_); hps.append(hp_)
        kss.append(ks_); vss.append(vs_); css.append(cs_); hss.append(hs_)
    return (xp, xs, jnp.stack(kps), jnp.stack(vps), jnp.stack(cps), jnp.stack(hps),
            jnp.stack(kss), jnp.stack(vss), jnp.stack(css), jnp.stack(hss))
```

```python
import numpy as np
import concourse.bass as bass
import concourse.mybir as mybir
from concourse.bass_utils import run_bass_kernel_spmd

F32 = mybir.dt.float32
BF16 = mybir.dt.bfloat16
AF = mybir.ActivationFunctionType
ALU = mybir.AluOpType

NCORES = 8
D = 1024
KC = 8
TP = 2048
TS = 256
NCH = 10
NHT = 22
EPS = 1e-6
CHW = 4096
NHALF_ = 1
TABWIN = 0.0
SLACK = 0.0
STREAMED = ["w_xr", "w_xy", "w_q", "w_k", "w_kv", "w_ao", "w_ro", "w_g", "w_out", "w_gu", "w_dn"]
NWB = 5


class Buf:
    __slots__ = ("name", "w", "r", "excl")

    def __init__(self, name="", excl=False):
        self.name = name
        self.w = None
        self.r = {}
        self.excl = excl


class Builder:
    EPOCH = 8000

    def __init__(self, nc):
        self.nc = nc
        self.eng = {"pe": nc.tensor, "act": nc.scalar, "dve": nc.vector,
                    "pool": nc.gpsimd, "sp": nc.sync}
        self.cnt = {e: 0 for e in self.eng}
        self.esems = {e: [] for e in self.eng}
        self.seen = {e: {} for e in self.eng}
        self.slots = []
        self.nwait = 0
        self.hist = {}
        self.rec = []
        self.tags = []
        self.tag = ""
        self.sched = []
        self.recording = True
        self.sim_time = 0.0
        self.n_tab = 0

    def new_slot(self, name):
        self.slots.append({"sem": self.nc.alloc_semaphore(name), "count": 0})
        return len(self.slots) - 1

    def _esem(self, e, epoch):
        while len(self.esems[e]) <= epoch:
            self.esems[e].append(self.nc.alloc_semaphore(f"s_{e}_{len(self.esems[e])}"))
        return self.esems[e][epoch]

    def _wait(self, e, tok):
        kind, key, val = tok
        if self.seen[e].get((kind, key), -1) >= val:
            return
        self.seen[e][(kind, key)] = val
        for k, v in self.hist.get(tok, {}).items():
            if self.seen[e].get(k, -1) < v:
                self.seen[e][k] = v
        if kind == "e":
            sem = self._esem(key, val // self.EPOCH)
            v = val % self.EPOCH + 1
        else:
            sem = self.slots[key]["sem"]
            v = val
        self.eng[e].wait_ge(sem, v)
        self.nwait += 1

    def emit(self, e, fns, reads=(), writes=(), slot=None, guards=()):
        if self.recording:
            self.rec.append((e, fns if isinstance(fns, (list, tuple)) else [fns], tuple(reads), tuple(writes), slot,
                             tuple(guards)))
            self.tags.append(self.tag)
            return None
        return self._emit_now(e, fns, reads, writes, slot, guards)

    def flush(self):
        import heapq
        ops, self.rec = self.rec, []
        tags, self.tags = self.tags, []
        n = len(ops)
        lastw, readers = {}, {}
        deps = [set() for _ in range(n)]
        for i, (e, fns, reads, writes, slot, guards) in enumerate(ops):
            wl = list(writes) + [b for b in reads if b.excl]
            for b in reads:
                if id(b) in lastw:
                    deps[i].add(lastw[id(b)])
            for b in wl + list(guards):
                if id(b) in lastw:
                    deps[i].add(lastw[id(b)])
                deps[i].update(readers.get(id(b), ()))
            for b in list(reads) + list(guards):
                readers.setdefault(id(b), set()).add(i)
            for b in wl:
                lastw[id(b)] = i
                readers[id(b)] = set()
            deps[i].discard(i)
        users = [[] for _ in range(n)]
        indeg = [len(d) for d in deps]
        for i, d in enumerate(deps):
            for j in d:
                users[j].append(i)
        cost = [sum(getattr(f, "cost", 0.3) for f in op[1]) for op in ops]
        tab = [getattr(op[1][-1], "tab", None) for op in ops]
        blev = [0.0] * n
        for i in range(n - 1, -1, -1):
            m = 0.0
            for u in users[i]:
                if blev[u] > m:
                    m = blev[u]
            blev[i] = cost[i] + 0.1 + m
        fin = [0.0] * n
        ready = [0.0] * n
        rdy = {e: [] for e in self.eng}
        for i in range(n):
            if indeg[i] == 0:
                rdy[ops[i][0]].append(i)
        et = {e: 0.0 for e in self.eng}
        cur_tab = None
        slot_free = {}
        order = []
        done = 0
        while done < n:
            best = None
            for e, lst in rdy.items():
                if not lst:
                    continue
                for i in lst:
                    sl_ = ops[i][4]
                    if sl_ is not None and slot_free.get(sl_, 0.0) > ready[i]:
                        ready[i] = slot_free[sl_]
                t0 = max(et[e], min(ready[i] for i in lst)) + SLACK
                bi, bk = None, None
                win = t0 + (TABWIN if e == "act" else 0.0)
                for i in lst:
                    if ready[i] <= t0 + 1e-9 or (e == "act" and ready[i] <= win and
                                                 not (tab[i] is not None and tab[i] != cur_tab)):
                        pen = 1.3 if (e == "act" and tab[i] is not None and tab[i] != cur_tab) else 0.0
                        k = (pen > 0 and any(ready[j] <= win + 1e-9 and not (tab[j] is not None and tab[j] != cur_tab)
                                             for j in lst if j != i), -blev[i], i)
                        if e == "pool" and ops[i][4] is not None:
                            k = (False, 0.0, i)
                        if bk is None or k < bk:
                            bi, bk = i, k
                if best is None or (t0, bi) < (best[0], best[2]):
                    best = (t0, e, bi)
            t, e, i = best
            t = max(et[e], ready[i])
            rdy[e].remove(i)
            isdma = ops[i][4] is not None
            c = cost[i]
            if e == "act" and tab[i] is not None:
                if tab[i] != cur_tab:
                    c += 1.3
                    self.n_tab += 1
                cur_tab = tab[i]
            et[e] = t + (0.15 if isdma else c)
            fin[i] = t + c + 0.2
            if isdma:
                slot_free[ops[i][4]] = fin[i]
            self.sched.append((tags[i], e, t, fin[i]))
            order.append(i)
            done += 1
            for u in users[i]:
                ready[u] = max(ready[u], fin[i])
                indeg[u] -= 1
                if indeg[u] == 0:
                    rdy[ops[u][0]].append(u)
        self.sim_time = max(fin) if n else 0.0
        self.recording = False
        for i in order:
            e, fns, reads, writes, slot, guards = ops[i]
            self._emit_now(e, fns, reads, writes, slot, guards)
        self.recording = True

    def _emit_now(self, e, fns, reads=(), writes=(), slot=None, guards=()):
        deps = {}
        xr = [b for b in reads if b.excl]
        if xr:
            writes = list(writes) + xr

        def add(tok):
            if tok is None:
                return
            k = (tok[0], tok[1])
            if deps.get(k, -1) < tok[2]:
                deps[k] = tok[2]

        for b in reads:
            add(b.w)
        rset = set(id(b) for b in reads)
        for b in writes:
            if b.w is not None:
                add(b.w)
            for k, v in b.r.items():
                add((k[0], k[1], v))
        for b in guards:
            add(b.w)
            for k, v in b.r.items():
                add((k[0], k[1], v))
        if slot is not None:
            sl = self.slots[slot]
            if sl["count"] > 0:
                add(("d", slot, sl["count"] * 16))
        for (kind, key), val in deps.items():
            if kind == "e" and key == e and e == "pe":
                continue
            self._wait(e, (kind, key, val))
        if not isinstance(fns, (list, tuple)):
            fns = [fns]
        ins = None
        for f in fns:
            ins = f(self.eng[e])
        if slot is not None:
            sl["count"] += 1
            ins.then_inc(sl["sem"], 16)
            tok = ("d", slot, sl["count"] * 16)
        else:
            g = self.cnt[e]
            self.cnt[e] += 1
            ins.then_inc(self._esem(e, g // self.EPOCH), 1)
            tok = ("e", e, g)
        snap = dict(self.seen[e])
        if tok[0] == "e" and tok[2] > 0:
            snap[("e", e)] = tok[2] - 1
        self.hist[tok] = snap
        for b in list(reads) + list(guards):
            b.r[(tok[0], tok[1])] = tok[2]
        for b in writes:
            b.w = tok
            b.r = {}
        return tok

    def barrier(self):
        self.flush()
        toks = [("e", e, self.cnt[e] - 1) for e in self.eng if self.cnt[e] > 0]
        toks += [("d", i, sl["count"] * 16) for i, sl in enumerate(self.slots) if sl["count"] > 0]
        for e in self.eng:
            for t in toks:
                if t[0] == "e" and t[1] == e:
                    continue
                self._wait(e, t)


def _fd(ap):
    n = 1
    for d in ap.shape[1:]:
        n *= d
    return n


def _c(f, cost):
    f.cost = cost
    return f


def MM(out, lhsT, rhs, start, stop):
    return _c(lambda pe: pe.matmul(out, lhsT, rhs, start=start, stop=stop), max(0.06, _fd(out) / 2400.0 + 0.02))


def TR(out, in_, ident):
    return _c(lambda pe: pe.transpose(out, in_, ident), 0.1)


def ACTF(out, in_, func, bias=None, scale=1.0, accum_out=None):
    kw = {}
    if bias is not None:
        kw["bias"] = bias
    if accum_out is not None:
        kw["accum_out"] = accum_out
    f = _c(lambda a: a.activation(out, in_, func, scale=scale, **kw), _fd(out) / 1600.0 + 0.2)
    f.tab = {AF.Tanh: "E", AF.Exp: "E", AF.Sqrt: "S", AF.Silu: "L", AF.Ln: "N"}.get(func)
    return f


def TT(out, in0, in1, op):
    return _c(lambda v: v.tensor_tensor(out, in0, in1, op), _fd(out) / 1650.0 + 0.17)


def TS_(out, in0, s1, s2, op0, op1=None):
    if op1 is None:
        return _c(lambda v: v.tensor_scalar(out, in0, s1, None, op0), _fd(out) / 1650.0 + 0.17)
    return _c(lambda v: v.tensor_scalar(out, in0, s1, s2, op0, op1), _fd(out) / 1650.0 + 0.17)


def PLTS(out, in0, s1, s2, op0, op1):
    return _c(lambda v: v.tensor_scalar(out, in0, s1, s2, op0, op1), _fd(out) / 850.0 + 0.2)


def STT(out, in0, scalar, in1, op0, op1):
    return _c(lambda v: v.scalar_tensor_tensor(out, in0, scalar, in1, op0, op1), _fd(out) / 1650.0 + 0.17)


def CP(out, in_):
    return _c(lambda v: v.tensor_copy(out, in_), _fd(out) / 1650.0 + 0.17)


def DMA(out, in_):
    n = 1
    for d in out.shape:
        n *= d
    nb = n * (2 if out.dtype == BF16 else 4)
    return _c(lambda q: q.dma_start(out=out, in_=in_), 1.5 + nb / 250e3)


IN_SPECS = {
    "xm": [TP, D], "xp": [TP, D], "xs": [TS, D],
    "cT": [128, KC, 5], "flag": [128, 1],
    "ck": [4, 128, 2, 128], "cv": [4, 128, 256],
    "sconv": [128, NCH, 4, 3], "sh0": [128, NCH, 4],
    "tab0": [128, 16, 128],
    "w_ada": [128, KC, 6144], "b_ada5": [5, 6144],
    "w_q": [2, 128, KC, 512], "w_k": [1, 128, KC, 256], "w_xy": [5, 128, KC, 512],
    "w_kv": [1, 128, KC, 512], "w_g": [4, 128, KC, 512], "w_xr": [5, 128, KC, 256],
    "w_ao": [2, 128, KC, 512], "w_ro": [4, 128, NCH, 256], "w_out": [2, 128, KC, 512],
    "w_gu": [11, 128, KC, 512], "w_dn": [8, 128, 11, 256],
    "w_rga": [128, NCH, 128], "w_rgx": [128, NCH, 128],
    "gpre": [128, 2, KC], "gpost5": [5, 2, 1024],
    "sinks": [128, 16], "wconv": [128, NCH, 4], "bconv": [128, NCH],
    "ba": [128, NCH], "bx": [128, NCH], "lam": [128, NCH],
    "tab": [128, 2, 16, 128], "ident": [128, 128], "sel": [5, 3, 128],
    "ckn": [4, 128, 256],
}
OUT_SPECS = {
    "y_m": [TP, D], "y_s": [TS, D],
    "kp": [128, 256], "vp": [128, 256], "convp": [128, NCH, 3], "hp": [128, NCH],
    "ks": [4, 128, 256], "vs": [4, 128, 256], "convs": [128, NCH, 4, 3], "hs": [128, NCH, 4],
}


def build_nc(stage=99, dbg=None):
    nc = bass.Bass("TRN2", target_bir_lowering=False)
    I = {k: nc.dram_tensor(k, s, F32, kind="ExternalInput").ap() for k, s in IN_SPECS.items()}
    O = {k: nc.dram_tensor(k, s, F32, kind="ExternalOutput").ap() for k, s in OUT_SPECS.items()}
    DBG = {}
    if dbg:
        for k, s in dbg.items():
            DBG[k] = nc.dram_tensor("dbg_" + k, s, F32, kind="ExternalOutput").ap()
    B = Builder(nc)
    I_out_m = [O['y_m'][k * 512:(k + 1) * 512, :] for k in range(4)]

    def sb(name, shape, dt):
        return nc.alloc_sbuf_tensor(name, shape, dt)

    identf = sb("identf", [8, 8], F32)
    identb = sb("identb", [128, 128], BF16)
    TAB = sb("TAB", [128, 2, 16, 128], BF16)
    ESINK = sb("ESINK", [128, 16], F32)
    G1 = sb("G1", [128, 2, KC, 5], F32)
    SH = sb("SHm", [128, 2, KC, 5], F32)
    GPRE = sb("GPRE", [128, 2, KC], F32)
    GG = sb("GG", [128, 2, 1024], F32)
    TAB0 = GG[:, 0, :].bitcast(BF16)[:, 0:2048].rearrange("p (h q) -> p h q", h=16)
    WCONV = sb("WCONV", [128, NCH, 4], F32)
    BCONV = sb("BCONV", [128, NCH], F32)
    BAH = sb("BAH", [128, NCH], F32)
    BXH = sb("BXH", [128, NCH], F32)
    CCH = sb("CCH", [128, NCH], F32)
    WRGA = sb("WRGA", [128, NCH, 128], BF16)
    WRGX = sb("WRGX", [128, NCH, 128], BF16)
    HC = sb("HC", [128, NCH], F32)
    CONVH = sb("CONVH", [128, NCH, 3], F32)
    FLAG = sb("FLAG", [128, 1], F32)
    SCONV = sb("SCONV", [128, NCH, 4, 3], F32)
    SH0 = sb("SH0", [128, NCH, 4], F32)
    CONVS = sb("CONVS", [128, NCH, 4, 3], F32)
    HS = sb("HS", [128, NCH, 4], F32)
    SS = sb("SS", [128, 16], F32)
    NHALF = sb("NHALF", [128, 16], F32)
    RSTD = sb("RSTD", [128, 16], F32)
    X = sb("X", [128, 4, D], F32)
    UTa = sb("UTa", [128, KC, 512], BF16)
    UTb = sb("UTb", [128, KC, 512], BF16)
    XS = sb("XS", [128, 2, D], F32)
    XN = sb("XN", [128, 4, D], BF16)
    WB = [sb(f"WB{i}", [128, CHW], BF16) for i in range(NWB)]
    KT = sb("KT", [128, 2, 640], BF16)
    VA = sb("VA", [128, 5, 4, 66], BF16)
    KCT = sb("KCT", [128, 4, 2, 128], BF16)
    VAC = sb("VAC", [128, 4, 4, 66], BF16)
    XRb2 = [sb(f"XRb{i}", [128, 516], F32) for i in range(2)]
    XC2 = [sb(f"XC{i}", [128, 512], F32) for i in range(2)]
    XCB2 = [sb(f"XCB{i}", [128, 512], BF16) for i in range(2)]
    CT2 = [None, None]
    KVO = sb("KVO", [128, 512], F32)
    GGROW = XS[0:5, :, :]
    SEL = KVO[0:5, 0:384].rearrange("p (a n) -> p a n", a=3)
    RA2 = [sb(f"RA{i}", [128, 512], F32) for i in range(2)]
    IX2 = [sb(f"IX{i}", [128, 512], F32) for i in range(2)]
    SB2 = [sb(f"SBt{i}", [128, 512], F32) for i in range(2)]
    HH2 = [sb(f"HH{i}", [128, 512], F32) for i in range(2)]
    GT2 = [sb(f"GT{i}", [128, 512], F32) for i in range(2)]
    GT = GT2[0]
    DEN = sb("DEN", [128, 16], F32)
    REC = sb("REC", [128, 16], F32)
    ON = sb("ON", [128, D], BF16)
    TMPY = XN[:, 0:2, :].rearrange("p a d -> p (a d)").bitcast(F32)
    ARENA = sb("ARENA", [128, 25600], BF16)
    QT = ARENA[:, 0:4096].rearrange("p (k t) -> p k t", k=8)
    AOT = ARENA[:, 4096:8192].rearrange("p (k t) -> p k t", k=8)
    MT = ARENA[:, 8192:12288].rearrange("p (k t) -> p k t", k=8)
    PT = ARENA[:, 12288:20480].rearrange("p (t h q) -> p t h q", t=2, h=16)
    ROT = ARENA[:, 20480:25600].rearrange("p (k t) -> p k t", k=NCH)
    HT = ARENA[:, 0:11264].rearrange("p (k t) -> p k t", k=NHT)
    Fb = ARENA[:, 11264:19456].bitcast(F32).rearrange("p (i d) -> p i d", i=4)
    T2 = sb("T2", [128, 512], F32)

    PSW = [nc.alloc_psum_tensor(f"PSW{i}", [128, 1024], F32) for i in range(4)]
    pslots = []
    for i in range(4):
        for h in range(2):
            pslots.append((PSW[i][:, h * 512:(h + 1) * 512], Buf(f"psw{i}{h}", True)))
    st = {"wb": 0, "ev": 0, "oslot": 0, "cv": 0}

    class Pool_:
        def __init__(self, banks):
            self.banks = list(banks)
            self.i = 0
            self.pairs = [p for p in range(4) if 2 * p in self.banks and 2 * p + 1 in self.banks]
            self.pi = 0

        def ps(self):
            k = self.banks[self.i % len(self.banks)]
            self.i += 1
            return pslots[k]

        def psb(self):
            ap, buf = self.ps()
            return ap.bitcast(BF16), buf

        def psw(self):
            p = self.pairs[self.pi % len(self.pairs)]
            self.pi += 1
            return PSW[p], [pslots[2 * p][1], pslots[2 * p + 1][1]]

    PALL = Pool_(range(8))
    cur = {"pool": PALL, "UT": None, "UTb": None}

    def next_ps():
        return cur["pool"].ps()

    def next_psb():
        return cur["pool"].psb()

    def next_psw():
        return cur["pool"].psw()

    wb_bufs = [Buf(f"wb{i}") for i in range(NWB)]
    wb_slots = [B.new_slot(f"d_wb{i}") for i in range(NWB)]
    ld_slots = [B.new_slot(f"d_ld{i}") for i in range(4)]
    pl_slots = [B.new_slot(f"d_pl{i}") for i in range(4)]
    wa_slots = [B.new_slot(f"d_wa{i}") for i in range(NWB)]
    st_slots = [B.new_slot(f"d_st{i}") for i in range(4)]

    WS = {n: nc.dram_tensor("ws_" + n, IN_SPECS[n], BF16, kind="Internal").ap() for n in STREAMED}
    ws_bufs = {n: [Buf(f"ws_{n}{i}") for i in range(IN_SPECS[n][0])] for n in STREAMED}
    GGS = nc.dram_tensor("ggs", [3, 128, 2, 1024], F32, kind="Internal").ap()
    ggs_bufs = [Buf(f"ggs{i}") for i in range(3)]
    cv_slots = [B.new_slot(f"d_cv{i}") for i in range(4)]

    def convert_weights(names):
        for n in names:
            for ch in range(IN_SPECS[n][0]):
                j = st["cv"]
                st["cv"] += 1
                f = DMA(WS[n][ch], I[n][ch])
                f.cost = 2.0 * f.cost + 3.0 * (f.cost - 1.5)
                B.emit("pool", f, writes=[ws_bufs[n][ch]], slot=cv_slots[j % 2])
                yield

    def load_w(name, ch=0, cx=None, bi=None):
        if name == "w_ada":
            src3 = I["w_ada"][:, :, ch * 512:(ch + 1) * 512]
            k, n = KC, 512
        else:
            src3 = WS[name][ch]
            k, n = IN_SPECS[name][2], IN_SPECS[name][3]
        if bi is not None:
            i = bi
        elif cx is None or cx.wbs is None:
            i = st["wb"] % NWB
            st["wb"] += 1
        else:
            i = cx.wbs[cx.wbi % len(cx.wbs)]
            cx.wbi += 1
        view = WB[i][:, 0:k * n].rearrange("p (k n) -> p k n", k=k)
        if name == "w_ada":
            B.emit("pool", DMA(view, src3), writes=[wb_bufs[i]], slot=wa_slots[i])
        else:
            B.emit("sp", DMA(view, src3), reads=[ws_bufs[name][ch]], writes=[wb_bufs[i]], slot=wb_slots[i])
        return view, wb_bufs[i]

    ldc = {"n": 0}

    def load(dst, src, bufs, eng="sp"):
        s = (pl_slots if eng == "pool" else ld_slots)[ldc["n"] % 4]
        ldc["n"] += 1
        B.emit(eng, DMA(dst, src), writes=bufs, slot=s)

    def store(dst, src, bufs):
        s = st_slots[st["oslot"] % 4]
        st["oslot"] += 1
        B.emit("sp", DMA(dst, src), reads=bufs, slot=s)

    def ev_eng():
        st["ev"] += 1
        return "act" if st["ev"] % 2 else "dve"

    b = {n: Buf(n) for n in [
        "ident", "TAB", "TAB0", "ESINK", "G1", "SH", "GPRE", "GGROW", "SEL", "GG", "rgc",
        "HC", "CONVH", "FLAG", "SCONV", "SH0", "CONVS", "HS", "SS", "RSTD", "X", "UT", "XN",
        "KT", "VA", "KCT", "VAC", "KVO", "RA", "IX", "SB", "HH",
        "GT", "DEN", "REC", "ON", "TMPY", "QT", "ROT", "AOT", "MT", "PT", "T1", "T2",
        "HT", "F", "MOD", "SC", "MODT", "NHALF"]}
    xb = [Buf(f"X{i}") for i in range(4)]

    def hb(name, hf):
        k = f"{name}{hf}"
        if k not in b:
            b[k] = Buf(k)
        return b[k]


    load(identf[:], I["ident"][0:8, 0:8], [b["ident"]])
    B.emit("pool", DMA(identb[:], I["ident"]), writes=[b["ident"]], slot=B.new_slot("d_c0"))
    cslot = B.new_slot("d_c1")
    B.emit("pool", DMA(TAB[:], I["tab"]), writes=[b["TAB"]], slot=cslot)
    B.emit("pool", DMA(WRGA[:], I["w_rga"]), writes=[b["rgc"]], slot=cslot)
    B.emit("pool", DMA(WRGX[:], I["w_rgx"]), writes=[b["rgc"]], slot=cslot)
    load(ESINK[:], I["sinks"], [b["ESINK"]])
    load(GPRE[:], I["gpre"], [b["GPRE"]])
    load(SEL[:], I["sel"], [b["SEL"]])
    load(WCONV[:], I["wconv"], [b["rgc"]])
    load(BCONV[:], I["bconv"], [b["rgc"]])
    load(BAH[:], I["ba"], [b["rgc"]])
    load(BXH[:], I["bx"], [b["rgc"]])
    load(CCH[:], I["lam"], [b["rgc"]])
    load(FLAG[:], I["flag"], [b["FLAG"]])
    load(SCONV[:], I["sconv"], [b["SCONV"]])
    load(SH0[:], I["sh0"], [b["SH0"]])
    MOD = ARENA[0:5, 0:12288].bitcast(F32)
    BADA = ARENA[0:5, 12288:24576].bitcast(F32)
    GPOST = XN[0:5, :, :].rearrange("p a d -> p (a d)").bitcast(F32).rearrange("p (a d) -> p a d", a=2)
    CT = T2[:, 160:200].rearrange("p (k s) -> p k s", k=KC)
    SC = T2[:, 200:220].bitcast(BF16).rearrange("p (k s) -> p k s", k=KC)
    MODT = T2[:, 0:160].rearrange("p (w k s) -> p w k s", w=4, k=KC)
    load(BADA[:], I["b_ada5"], [b["MOD"]])
    load(GPOST[:], I["gpost5"], [b["GGROW"]])
    load(CT[:], I["cT"], [b["SC"]])

    if stage == 0.1:
        B.barrier()
        return nc, B
    B.emit("act", ACTF(ESINK[:], ESINK[:], AF.Exp), reads=[b["ESINK"]], writes=[b["ESINK"]])
    B.emit("act", ACTF(CCH[:], CCH[:], AF.Exp, scale=-1.0), reads=[b["rgc"]], writes=[b["rgc"]])
    B.emit("act", ACTF(CCH[:], CCH[:], AF.Ln, bias=1.0), reads=[b["rgc"]], writes=[b["rgc"]])
    B.emit("dve", TS_(CCH[:], CCH[:], -4.0, None, ALU.mult), reads=[b["rgc"]], writes=[b["rgc"]])
    B.emit("dve", TS_(BAH[:], BAH[:], 0.5, None, ALU.mult), reads=[b["rgc"]], writes=[b["rgc"]])
    B.emit("dve", TS_(BXH[:], BXH[:], 0.5, None, ALU.mult), reads=[b["rgc"]], writes=[b["rgc"]])

    if stage == 0.2:
        B.barrier()
        return nc, B
    B.emit("act", ACTF(SC[:], CT[:], AF.Silu), reads=[b["SC"]], writes=[b["SC"]])
    for j in range(12):
        wv, wbuf = load_w("w_ada", j)
        ps, pb = next_ps()
        B.emit("pe", [MM(ps[0:5, :], SC[:, kc, :], wv[:, kc, :], kc == 0, kc == KC - 1)
                      for kc in range(KC)], reads=[b["SC"], wbuf], writes=[pb])
        B.emit("dve", TT(MOD[:, j * 512:(j + 1) * 512], ps[0:5, :], BADA[:, j * 512:(j + 1) * 512],
                         ALU.add), reads=[pb, b["MOD"]], writes=[b["MOD"]])
    if stage == 0.3:
        B.barrier()
        return nc, B
    ps, pb = next_ps()
    fns = []
    for w, col in enumerate([0, 1, 3, 4]):
        for kc in range(KC):
            o = (w * KC + kc) * 5
            fns.append(TR(ps[:, o:o + 5], MOD[0:5, col * 1024 + kc * 128: col * 1024 + (kc + 1) * 128],
                          identf[0:5, 0:5]))
    B.emit("pe", fns, reads=[b["MOD"], b["ident"]], writes=[pb])
    B.emit("dve", CP(MODT[:].rearrange("p w k s -> p (w k s)"), ps[:, 0:160]),
           reads=[pb], writes=[b["MODT"]])
    for mf in range(2):
        for s in range(5):
            B.emit("dve", STT(G1[:, mf, :, s], MODT[:, 2 * mf + 1, :, s], 1.0, GPRE[:, mf, :],
                              ALU.add, ALU.mult),
                   reads=[b["MODT"], b["GPRE"]], writes=[b["G1"]])
        B.emit("dve", CP(SH[:, mf, :, :], MODT[:, 2 * mf, :, :]), reads=[b["MODT"]], writes=[b["SH"]])
        B.emit("dve", TT(GGROW[:, mf, :], MOD[:, (2 + 3 * mf) * 1024:(3 + 3 * mf) * 1024],
                         GPOST[:, mf, :], ALU.mult), reads=[b["MOD"], b["GGROW"]], writes=[b["GGROW"]])

    if stage == 0.4:
        B.barrier()
        return nc, B

    class Cx:
        def __init__(self, pool, UT, ub, ssc, ybanks=None, wbs=None):
            self.pool, self.UT, self.ub, self.ssc, self.ybanks = pool, UT, ub, ssc, ybanks
            self.wbs, self.wbi = wbs, 0

    ubA, ubB = Buf("UTa"), Buf("UTb")
    ssb = [Buf(f"ss{i}") for i in range(16)]

    for g_ in range(4):
        b[f"PT{g_}"] = Buf(f"PT{g_}")
    PTS = [b[f"PT{g_}"] for g_ in range(4)]

    def wr(name):
        al = {"QT": ["HT"], "AOT": ["HT"], "MT": ["HT", "F"], "HT": ["QT", "AOT", "MT"],
              "F": ["MT"] + [f"PT{g_}" for g_ in range(4)]}
        for g_ in range(4):
            al[f"PT{g_}"] = ["F"]
        return dict(writes=[b[name]], guards=[b[a] for a in al.get(name, [])])

    def build_gg(cx, selidx):
        for mf in range(2):
            for h in range(2):
                ps, pb = cx.pool.ps()
                B.emit("pe", MM(ps[:, :], SEL[:, selidx, :], GGROW[:, mf, h * 512:(h + 1) * 512], True, True),
                       reads=[b["SEL"], b["GGROW"]], writes=[pb])
                B.emit("dve", CP(GG[:, mf, h * 512:(h + 1) * 512], ps[:, :]), reads=[pb], writes=[b["GG"], b["TAB0"]])

    for v_ in range(3):
        build_gg(Cx(PALL, None, None, 0), v_)
        B.emit("sp", DMA(GGS[v_], GG[:]), reads=[b["GG"]], writes=[ggs_bufs[v_]], slot=st_slots[v_])

    def load_gg(v_):
        s_ = ld_slots[ldc["n"] % 4]
        ldc["n"] += 1
        B.emit("sp", DMA(GG[:], GGS[v_]), reads=[ggs_bufs[v_]], writes=[b["GG"], b["TAB0"]], slot=s_)

    def rstd_from_ss(c0, nt, eps, sbuf=None):
        sb_ = ssb[c0:c0 + nt]
        B.emit("pool", TS_(RSTD[:, c0:c0 + nt], SS[:, c0:c0 + nt], 1.0 / D, eps, ALU.mult, ALU.add),
               reads=sb_, writes=sb_)
        B.emit("pool", TT(RSTD[:, c0:c0 + nt], RSTD[:, c0:c0 + nt], NHALF[:, 0:nt], ALU.pow),
               reads=sb_ + [b["NHALF"]], writes=sb_)

    xsb = [Buf("XS0"), Buf("XS1")]
    xnb = [Buf(f"XN{i}") for i in range(4)]
    b["KVO"] = Buf("KVO")

    def transposes_to_UT(cx, nt, mf, segs):
        for kc in range(KC):
            ps, pb = cx.pool.psb()
            B.emit("pe", [TR(ps[:, i * 128:(i + 1) * 128], XN[:, i, kc * 128:(kc + 1) * 128], identb[:])
                          for i in range(nt)], reads=xnb[0:nt] + [b["ident"]], writes=[pb])
            for (c0, n, s) in segs:
                e = ev_eng()
                if e == "act":
                    B.emit("act", ACTF(cx.UT[:, kc, c0:c0 + n], ps[:, c0:c0 + n], AF.Identity,
                                       bias=SH[:, mf, kc, s:s + 1], scale=G1[:, mf, kc, s:s + 1]),
                           reads=[pb, b["G1"], b["SH"]], writes=[cx.ub])
                else:
                    B.emit("dve", TS_(cx.UT[:, kc, c0:c0 + n], ps[:, c0:c0 + n], G1[:, mf, kc, s:s + 1],
                                      SH[:, mf, kc, s:s + 1], ALU.mult, ALU.add),
                           reads=[pb, b["G1"], b["SH"]], writes=[cx.ub])
            yield

    def prenorm_stream(cx, src_rows, nt, segs):
        c0 = cx.ssc
        for i in range(nt):
            load(XS[:, i % 2, :], src_rows[i * 128:(i + 1) * 128, :], [xsb[i % 2]])
            B.emit("act", ACTF(XN[:, i, :], XS[:, i % 2, :], AF.Square, accum_out=SS[:, c0 + i:c0 + i + 1]),
                   reads=[xsb[i % 2]], writes=[xnb[i], ssb[c0 + i]])
            yield
        rstd_from_ss(c0, nt, EPS)
        for i in range(nt):
            load(XS[:, i % 2, :], src_rows[i * 128:(i + 1) * 128, :], [xsb[i % 2]])
            B.emit("pool", PLTS(XN[:, i, :], XS[:, i % 2, :], RSTD[:, c0 + i:c0 + i + 1], 0.0, ALU.mult, ALU.add),
                   reads=[xsb[i % 2], ssb[c0 + i]], writes=[xnb[i]])
            yield
        yield from transposes_to_UT(cx, nt, 0, segs)

    def prenorm_resident(cx, nt, segs):
        c0 = cx.ssc
        for i in range(nt):
            B.emit("act", ACTF(XN[:, i, :], X[:, i, :], AF.Square, accum_out=SS[:, c0 + i:c0 + i + 1]),
                   reads=[xb[i]], writes=[xnb[i], ssb[c0 + i]])
        rstd_from_ss(c0, nt, EPS)
        for i in range(nt):
            B.emit("pool", PLTS(XN[:, i, :], X[:, i, :], RSTD[:, c0 + i:c0 + i + 1], 0.0, ALU.mult, ALU.add),
                   reads=[xb[i], ssb[c0 + i]], writes=[xnb[i]])
        yield from transposes_to_UT(cx, nt, 1, segs)

    def load_x(src_rows, nt):
        for i in range(nt):
            load(X[:, i, :], src_rows[i * 128:(i + 1) * 128, :], [xb[i]])

    B.emit("dve", lambda v: v.memset(VA[:, :, :, 64:66], 1.0), writes=[b["VA"]])
    B.emit("dve", lambda v: v.memset(VAC[:, :, :, 64:66], 1.0), writes=[b["VAC"]])
    B.emit("dve", lambda v: v.memset(HC[:], 0.0), writes=[hb("HC", c_) for c_ in range(NCH)])
    B.emit("dve", lambda v: v.memset(NHALF[:], -0.5), writes=[b["NHALF"]])
    B.emit("dve", lambda v: v.memset(CONVH[:], 0.0), writes=[hb("CONVH", c_) for c_ in range(NCH)])
    B.barrier()
    for _ in convert_weights(STREAMED):
        pass

    def v3(ap, a):
        return ap.rearrange("p (a n) -> p a n", a=a)

    def qk_phase(cx, T, k_only_cols=None):
        if k_only_cols is None:
            for ch in range(2):
                wv, wbuf = load_w("w_q", ch, cx)
                for tl in range(4):
                    ps, pb = cx.pool.ps()
                    B.emit("pe", [MM(ps[:, 0:T], wv[:, kc, tl * 128:(tl + 1) * 128], cx.UT[:, kc, 0:T],
                                     kc == 0, kc == KC - 1) for kc in range(KC)],
                           reads=[cx.ub, wbuf], writes=[pb])
                    e = ev_eng()
                    fn = ACTF(QT[:, ch * 4 + tl, 0:T], ps[:, 0:T], AF.Copy) if e == "act" else \
                        CP(QT[:, ch * 4 + tl, 0:T], ps[:, 0:T])
                    B.emit(e, fn, reads=[pb], **wr("QT"))
                    yield
        wv, wbuf = load_w("w_k", 0, cx)
        for m in range(2):
            ps, pb = cx.pool.ps()
            B.emit("pe", [MM(ps[:, 0:T], wv[:, kc, m * 128:(m + 1) * 128], cx.UT[:, kc, 0:T],
                             kc == 0, kc == KC - 1) for kc in range(KC)],
                   reads=[cx.ub, wbuf], writes=[pb])
            if k_only_cols is None:
                B.emit("act", ACTF(KT[:, m, 128:128 + T], ps[:, 0:T], AF.Copy), reads=[pb], writes=[b["KT"]])
            else:
                c0 = k_only_cols
                B.emit("act", ACTF(KT[:, m, 0:128], ps[:, c0:c0 + 128], AF.Copy), reads=[pb], writes=[b["KT"]])
            yield

    def kv_tile(cx, w, tok0, ntok, va_tile, kout=None):
        wv, wbuf = w
        ps, pb = cx.pool.ps()
        B.emit("pe", [MM(ps[0:ntok, :], cx.UT[:, kc, tok0:tok0 + ntok], wv[:, kc, :], kc == 0, kc == KC - 1)
                      for kc in range(KC)], reads=[cx.ub, wbuf], writes=[pb])
        B.emit("act", ACTF(VA[0:ntok, va_tile, :, 0:64], v3(ps[0:ntok, 256:512], 4), AF.Copy),
               reads=[pb], writes=[b["VA"]])
        if kout is not None:
            B.emit("dve", CP(KVO[0:ntok, :], ps[0:ntok, :]), reads=[pb], writes=[b["KVO"]])
            store(kout[0], KVO[0:ntok, 0:256], [b["KVO"]])
            store(kout[1], KVO[0:ntok, 256:512], [b["KVO"]])

    def rnn_channel(cx, c, T, kind, ps_x, pbx, ps_y, pby):
        sample = (kind == "sample")
        H = T // NHALF_
        st_ = c % 2
        XRb, XC, XCB, CT = XRb2[st_], XC2[st_], XCB2[st_], CT2[st_]
        RA, IX, SB_, HH, GT = RA2[st_], IX2[st_], SB2[st_], HH2[st_], GT2[st_]
        xrB = hb("XRb", st_)
        if sample:
            XR3 = v3(XRb[:, 0:268], 4)
            B.emit("dve", CP(XR3[:, :, 0:3], SCONV[:, c, :, :]), reads=[b["SCONV"]], writes=[xrB])
            B.emit("act", ACTF(XR3[:, :, 3:67], v3(ps_x[:, 0:256], 4), AF.Copy), reads=[pbx], writes=[xrB])
            B.emit("dve", CP(CONVS[:, c, :, :], XR3[:, :, 64:67]), reads=[xrB], writes=[b["CONVS"]])
        else:
            B.emit("dve", CP(XRb[:, 0:3], CONVH[:, c, :]), reads=[hb("CONVH", c)], writes=[xrB])
            B.emit("act", ACTF(XRb[:, 3:3 + T], ps_x[:, 0:T], AF.Copy), reads=[pbx], writes=[xrB])
            B.emit("dve", CP(CONVH[:, c, :], XRb[:, T:T + 3]), reads=[xrB], writes=[hb("CONVH", c)])
        yield

        def half(hf):
            lo, hi = hf * H, (hf + 1) * H
            cs = slice(lo, hi)
            if sample:
                XR3 = v3(XRb[:, 0:268], 4)
                taps = [XR3[:, (4 // NHALF_) * hf:(4 // NHALF_) * (hf + 1), j:j + 64] for j in range(4)]
                xc_o, tmp_o = v3(XC[:, cs], 4 // NHALF_), None
            else:
                taps = [XRb[:, lo + j:hi + j] for j in range(4)]
                xc_o, tmp_o = XC[:, cs], None
            SBb, RAb, IXb, GTb = (hb(f"{n}s{st_}", hf) for n in ("SB", "RA", "IX", "GT"))
            XCb, XCBb, CTb = (hb(f"{n}s{st_}", hf) for n in ("XC", "XCB", "CT"))
            if False:
                B.emit("pool", TS_(xc_o, taps[3], WCONV[:, c, 3:4], BCONV[:, c:c + 1], ALU.mult, ALU.add),
                       reads=[xrB, b["rgc"]], writes=[XCb])
                for j in (2, 1, 0):
                    B.emit("pool", TS_(tmp_o, taps[j], WCONV[:, c, j:j + 1], 0.0, ALU.mult, ALU.add),
                           reads=[xrB, b["rgc"]], writes=[CTb])
                    B.emit("pool", TT(xc_o, xc_o, tmp_o, ALU.add), reads=[XCb, CTb], writes=[XCb])
            else:
                B.emit("pool", PLTS(xc_o, taps[3], WCONV[:, c, 3:4], BCONV[:, c:c + 1], ALU.mult, ALU.add),
                       reads=[xrB, b["rgc"]], writes=[XCb])
                for j in (2, 1, 0):
                    B.emit("dve", STT(xc_o, taps[j], WCONV[:, c, j:j + 1], xc_o, ALU.mult, ALU.add),
                           reads=[xrB, XCb, b["rgc"]], writes=[XCb])
            yield
            B.emit("dve", CP(XCB[:, cs], XC[:, cs]), reads=[XCb], writes=[XCBb])
            pg, pgb = cx.pool.ps()
            if NHALF_ == 2:
                pg2, pgb2, o2 = pg, pgb, 256
                B.emit("pe", [MM(pg[:, 0:H], WRGA[:, c, :], XCB[:, cs], True, True),
                              MM(pg[:, 256:256 + H], WRGX[:, c, :], XCB[:, cs], True, True)],
                       reads=[XCBb, b["rgc"]], writes=[pgb])
            else:
                pg2, pgb2 = cx.pool.ps()
                o2 = 0
                B.emit("pe", MM(pg[:, 0:H], WRGA[:, c, :], XCB[:, cs], True, True), reads=[XCBb, b["rgc"]], writes=[pgb])
                B.emit("pe", MM(pg2[:, 0:H], WRGX[:, c, :], XCB[:, cs], True, True), reads=[XCBb, b["rgc"]], writes=[pgb2])
            yield
            B.emit("act", ACTF(RA[:, cs], pg[:, 0:H], AF.Tanh, bias=BAH[:, c:c + 1], scale=0.5),
                   reads=[pgb, b["rgc"]], writes=[RAb])
            B.emit("act", ACTF(IX[:, cs], pg2[:, o2:o2 + H], AF.Tanh, bias=BXH[:, c:c + 1], scale=0.5),
                   reads=[pgb2, b["rgc"]], writes=[IXb])
            yield
            B.emit("act", ACTF(RA[:, cs], RA[:, cs], AF.Exp, bias=CCH[:, c:c + 1], scale=CCH[:, c:c + 1]),
                   reads=[RAb, b["rgc"]], writes=[RAb])
            B.emit("dve", STT(IX[:, cs], IX[:, cs], 1.0, XC[:, cs], ALU.add, ALU.mult), reads=[IXb, XCb], writes=[IXb])
            yield
            B.emit("dve", TT(SB_[:, cs], RA[:, cs], RA[:, cs], ALU.mult), reads=[RAb], writes=[SBb])
            if ps_y is not None:
                B.emit("act", ACTF(GT[:, cs], ps_y[:, cs], AF.Square), reads=[pby], writes=[GTb])
                B.emit("pool", PLTS(GT[:, cs], GT[:, cs], 0.044715, 1.0, ALU.mult, ALU.add), reads=[GTb], writes=[GTb])
                B.emit("dve", TT(GT[:, cs], GT[:, cs], ps_y[:, cs], ALU.mult), reads=[GTb, pby], writes=[GTb])
            yield
            if ps_y is not None:
                B.emit("act", ACTF(GT[:, cs], GT[:, cs], AF.Tanh, scale=0.7978845608028654), reads=[GTb], writes=[GTb])
                yield
                B.emit("dve", STT(GT[:, cs], GT[:, cs], 1.0, ps_y[:, cs], ALU.add, ALU.mult), reads=[GTb, pby], writes=[GTb])
            yield

        gens = [half(hf) for hf in range(NHALF_)]
        while gens:
            for g_ in list(gens):
                try:
                    next(g_)
                except StopIteration:
                    gens.remove(g_)
            yield
        for hf in range(NHALF_):
            cs = slice(hf * H, (hf + 1) * H)
            B.emit("act", ACTF(SB_[:, cs], SB_[:, cs], AF.Sqrt, bias=1.0, scale=-1.0), reads=[hb(f"SBs{st_}", hf)], writes=[hb(f"SBs{st_}", hf)])
        for hf in range(NHALF_):
            cs = slice(hf * H, (hf + 1) * H)
            B.emit("dve", STT(SB_[:, cs], SB_[:, cs], 0.5, IX[:, cs], ALU.mult, ALU.mult),
                   reads=[hb(f"SBs{st_}", hf), hb(f"IXs{st_}", hf)], writes=[hb(f"SBs{st_}", hf)])
        allab = [hb(f"RAs{st_}", h_) for h_ in range(NHALF_)] + [hb(f"SBs{st_}", h_) for h_ in range(NHALF_)]
        if sample:
            for s_ in range(4):
                B.emit("dve", lambda v, s_=s_: v.tensor_tensor_scan(
                    HH[:, s_ * 64:(s_ + 1) * 64], RA[:, s_ * 64:(s_ + 1) * 64], SB_[:, s_ * 64:(s_ + 1) * 64],
                    SH0[:, c, s_:s_ + 1], ALU.mult, ALU.add),
                    reads=allab + [b["SH0"]], writes=[hb("HHs", st_)])
            B.emit("dve", CP(HS[:, c, :], v3(HH[:, 0:256], 4)[:, :, 63]), reads=[hb("HHs", st_)], writes=[b["HS"]])
        else:
            B.emit("dve", lambda v: v.tensor_tensor_scan(HH[:, 0:T], RA[:, 0:T], SB_[:, 0:T], HC[:, c:c + 1],
                                                         ALU.mult, ALU.add),
                   reads=allab + [hb("HC", c)], writes=[hb("HHs", st_)])
            B.emit("dve", CP(HC[:, c:c + 1], HH[:, T - 1:T]), reads=[hb("HHs", st_)], writes=[hb("HC", c)])
        yield
        if ps_y is None:
            return
        B.emit("dve", STT(ROT[:, c, 0:T], GT[:, 0:T], 0.5, HH[:, 0:T], ALU.mult, ALU.mult),
               reads=[hb(f"GTs{st_}", h_) for h_ in range(NHALF_)] + [hb("HHs", st_)], writes=[b["ROT"]])
        yield

    def rnn_phase(cx, T, kind, with_y):
        active = []

        def step_all():
            for a_ in list(active):
                try:
                    next(a_[0])
                    a_[1] += 1
                except StopIteration:
                    active.remove(a_)

        for c in range(NCH):
            while len(active) >= 2 or (active and active[0][1] < 6):
                step_all()
                yield
            if with_y:
                if c % 2 == 0:
                    wv, wbuf = load_w("w_xy", c // 2, cx)
                cc = c % 2
                xcols, ycols = slice((2 * cc) * 128, (2 * cc + 1) * 128), slice((2 * cc + 1) * 128, (2 * cc + 2) * 128)
            else:
                if c % 2 == 0:
                    wv, wbuf = load_w("w_xr", c // 2, cx)
                xcols = slice((c % 2) * 128, (c % 2 + 1) * 128)
            ps_x, pbx = cx.pool.ps()
            B.emit("pe", [MM(ps_x[:, 0:T], wv[:, kc, xcols], cx.UT[:, kc, 0:T], kc == 0, kc == KC - 1)
                          for kc in range(KC)], reads=[cx.ub, wbuf], writes=[pbx])
            ps_y = pby = None
            if with_y:
                ps_y, pby = pslots[cx.ybanks[c % 2]] if cx.ybanks else cx.pool.ps()
                B.emit("pe", [MM(ps_y[:, 0:T], wv[:, kc, ycols], cx.UT[:, kc, 0:T], kc == 0, kc == KC - 1)
                              for kc in range(KC)], reads=[cx.ub, wbuf], writes=[pby])
            active.append([rnn_channel(cx, c, T, kind, ps_x, pbx, ps_y, pby), 0])
        while active:
            step_all()
            yield

    def attn_scores(cx, g, q0, nq, ktiles):
        m, e = g // 2, g % 2
        prt = slice(64 * e, 64 * e + 64)
        for t, kt in enumerate(ktiles):
            nk = kt["nk"]
            S, sbuf_ = cx.pool.ps()
            B.emit("pe", MM(v3(S[0:nk, 0:4 * nq], 4), kt["kT"](m)[prt, :], QT[prt, 4 * m:4 * m + 4, q0:q0 + nq],
                            True, True), reads=[b["KT"], b["KCT"], b["QT"]], writes=[sbuf_])
            B.emit("act", ACTF(PT[0:nk, t, 4 * g:4 * g + 4, 0:nq], v3(S[0:nk, 0:4 * nq], 4), AF.Exp, scale=0.125),
                   reads=[sbuf_], **wr(f"PT{g}"))
            B.emit("dve", TT(PT[0:nk, t, 4 * g:4 * g + 4, 0:nq], PT[0:nk, t, 4 * g:4 * g + 4, 0:nq],
                             kt["tab"][0:nk, 4 * g:4 * g + 4, 0:nq], ALU.mult),
                   reads=[b[f"PT{g}"], b["TAB"], b["TAB0"]], **wr(f"PT{g}"))

    def attn_pv(cx, g, nq, ktiles):
        Ob, obuf = cx.pool.ps()
        fns = []
        for j in range(4):
            h = 4 * g + j
            for t, kt in enumerate(ktiles):
                nk = kt["nk"]
                fns.append(MM(Ob[0:nq, 65 * j:65 * j + 65], PT[0:nk, t, h, 0:nq], kt["va"][0:nk, g, 0:65],
                              t == 0, t == len(ktiles) - 1))
        B.emit("pe", fns, reads=[b[f"PT{g}"], b["VA"], b["VAC"]], writes=[obuf])
        O3 = Ob[0:nq, 0:260].rearrange("p (h e) -> p h e", e=65)
        B.emit("dve", TT(DEN[0:nq, 4 * g:4 * g + 4], O3[:, :, 64], ESINK[0:nq, 4 * g:4 * g + 4], ALU.add),
               reads=[obuf, b["ESINK"]], writes=[hb("DEN", g)])
        B.emit("dve", lambda v, g=g: v.reciprocal(REC[0:nq, 4 * g:4 * g + 4], DEN[0:nq, 4 * g:4 * g + 4]),
               reads=[hb("DEN", g)], writes=[hb("REC", g)])
        ON3 = ON[0:nq, 256 * g:256 * g + 256].rearrange("p (h d) -> p h d", h=4)
        B.emit("dve", TT(ON3, O3[:, :, 0:64], REC[0:nq, 4 * g:4 * g + 4].unsqueeze(2).to_broadcast([nq, 4, 64]), ALU.mult),
               reads=[obuf, hb("REC", g)], writes=[b["ON"]])

    def attention(cx, q0, nq, ktiles):
        attn_scores(cx, 0, q0, nq, ktiles)
        for g in range(4):
            if g + 1 < 4:
                attn_scores(cx, g + 1, q0, nq, ktiles)
            attn_pv(cx, g, nq, ktiles)
            yield
        psb, pbb = cx.pool.psb()
        B.emit("pe", [TR(psb[:, kc * 128:kc * 128 + nq], ON[0:nq, kc * 128:(kc + 1) * 128], identb[0:nq, 0:nq])
                      for kc in range(KC)], reads=[b["ON"], b["ident"]], writes=[pbb])
        B.emit("act", ACTF(AOT[:, :, q0:q0 + nq], v3(psb[:, :], 8)[:, :, 0:nq], AF.Copy), reads=[pbb], **wr("AOT"))
        yield

    def merge_phase(cx, T):
        wao = wro = wg = None
        for ot in range(8):
            if ot % 4 == 0:
                mb = cx.wbs if cx.wbs is not None else [None, None, None]
                wao = load_w("w_ao", ot // 4, cx, bi=mb[0])
            if ot % 2 == 0:
                wro = load_w("w_ro", ot // 2, cx, bi=mb[1])
                wg = load_w("w_g", ot // 2, cx, bi=mb[2])
            o4, o2 = ot % 4, ot % 2
            pGa, bGa = cx.pool.ps()
            B.emit("pe", [MM(pGa[:, 0:T], wg[0][:, kc, (2 * o2) * 128:(2 * o2 + 1) * 128], cx.UT[:, kc, 0:T], kc == 0, kc == KC - 1)
                          for kc in range(KC)], reads=[cx.ub, wg[1]], writes=[bGa])
            pGr, bGr = cx.pool.ps()
            B.emit("pe", [MM(pGr[:, 0:T], wg[0][:, kc, (2 * o2 + 1) * 128:(2 * o2 + 2) * 128], cx.UT[:, kc, 0:T], kc == 0, kc == KC - 1)
                          for kc in range(KC)], reads=[cx.ub, wg[1]], writes=[bGr])
            pA, bA = cx.pool.ps()
            B.emit("pe", [MM(pA[:, 0:T], wao[0][:, kc, o4 * 128:(o4 + 1) * 128], AOT[:, kc, 0:T], kc == 0, kc == KC - 1)
                          for kc in range(KC)], reads=[b["AOT"], wao[1]], writes=[bA])
            pR, bR = cx.pool.ps()
            B.emit("pe", [MM(pR[:, 0:T], wro[0][:, c, o2 * 128:(o2 + 1) * 128], ROT[:, c, 0:T], c == 0, c == NCH - 1)
                          for c in range(NCH)], reads=[b["ROT"], wro[1]], writes=[bR])
            T1h, T2h = ON[:, :].bitcast(F32)[:, 0:T], T2[:, 0:T]
            t1b = [b["ON"]]
            B.emit("act", ACTF(T1h, pGa[:, 0:T], AF.Tanh, scale=0.5), reads=[bGa], writes=t1b)
            B.emit("act", ACTF(T2h, pGr[:, 0:T], AF.Tanh, scale=0.5), reads=[bGr], writes=[b["T2"]])
            B.emit("dve", STT(T1h, T1h, 1.0, pA[:, 0:T], ALU.add, ALU.mult), reads=t1b + [bA], writes=t1b)
            B.emit("dve", STT(T2h, T2h, 1.0, pR[:, 0:T], ALU.add, ALU.mult), reads=[b["T2"], bR], writes=[b["T2"]])
            B.emit("dve", TT(MT[:, ot, 0:T], T1h, T2h, ALU.add), reads=t1b + [b["T2"]], **wr("MT"))
            yield

    def tm_norm_residual(cx, i, src, src_bufs, mf, eps, inplace, out_rows=None):
        c = cx.ssc + i
        sbuf = ssb[c]
        B.emit("act", ACTF(ON[:, :], src, AF.Square, accum_out=SS[:, c:c + 1]), reads=src_bufs, writes=[b["ON"], sbuf])
        rstd_from_ss(c, 1, eps, sbuf)
        if inplace:
            dst, dbufs = src, src_bufs
        else:
            dst, dbufs = Fb[:, 1, :], [b["F"]]
        if inplace:
            B.emit("dve", STT(dst, src, RSTD[:, c:c + 1], GG[:, mf, :], ALU.mult, ALU.mult),
                   reads=src_bufs + [sbuf, b["GG"]], writes=dbufs)
        else:
            B.emit("dve", STT(dst, src, RSTD[:, c:c + 1], GG[:, mf, :], ALU.mult, ALU.mult),
                   reads=src_bufs + [sbuf, b["GG"]], **wr("F"))
        B.emit("dve", TT(X[:, i, :], dst, X[:, i, :], ALU.add), reads=dbufs + [xb[i]], writes=[xb[i]])
        if out_rows is not None:
            store(out_rows, X[:, i, :], [xb[i]])

    def wout_phase(cx, nt, sel_of_tile):
        w0 = load_w("w_out", 0, cx)
        w1 = load_w("w_out", 1, cx)
        for i in range(nt):
            if sel_of_tile is not None:
                load_gg(sel_of_tile[i])
            pw, pwb = cx.pool.psw()
            for h, w in enumerate((w0, w1)):
                B.emit("pe", [MM(pw[:, h * 512:(h + 1) * 512], MT[:, kc, i * 128:(i + 1) * 128], w[0][:, kc, :],
                                 kc == 0, kc == KC - 1) for kc in range(KC)], reads=[b["MT"], w[1]], writes=[pwb[h]])
            tm_norm_residual(cx, i, pw[:, :], list(pwb), 0, 4.0 * EPS, False)
            yield

    def ffn_phase(cx, nt, sel_of_tile, out_ap):
        T = nt * 128
        for ch in range(11):
            wv, wbuf = load_w("w_gu", ch, cx)
            for cc in range(2):
                ht = 2 * ch + cc
                pG, bG = cx.pool.ps()
                B.emit("pe", [MM(pG[:, 0:T], wv[:, kc, (2 * cc) * 128:(2 * cc + 1) * 128], cx.UT[:, kc, 0:T], kc == 0, kc == KC - 1)
                              for kc in range(KC)], reads=[cx.ub, wbuf], writes=[bG])
                yield
                pU, bU = cx.pool.ps()
                B.emit("pe", [MM(pU[:, 0:T], wv[:, kc, (2 * cc + 1) * 128:(2 * cc + 2) * 128], cx.UT[:, kc, 0:T], kc == 0, kc == KC - 1)
                              for kc in range(KC)], reads=[cx.ub, wbuf], writes=[bU])
                if ht % 2 == 0:
                    Tt, tb = T2[:, 0:T], b["T2"]
                else:
                    Tt, tb = ON[:, :].bitcast(F32)[:, 0:T], b["ON"]
                B.emit("act", ACTF(Tt, pG[:, 0:T], AF.Tanh, scale=0.5), reads=[bG], writes=[tb])
                B.emit("dve", STT(Tt, Tt, 1.0, pG[:, 0:T], ALU.add, ALU.mult), reads=[tb, bG], writes=[tb])
                B.emit("dve", STT(HT[:, ht, 0:T], Tt, 0.5, pU[:, 0:T], ALU.mult, ALU.mult), reads=[tb, bU], **wr("HT"))
                yield
        for cq in range(4):
            wa, wab = load_w("w_dn", 2 * cq, cx)
            wb_, wbb = load_w("w_dn", 2 * cq + 1, cx)
            for i in range(nt):
                ps, pb = cx.pool.ps()
                B.emit("pe", [MM(ps[:, 0:256], HT[:, ht, i * 128:(i + 1) * 128],
                                 (wa[:, ht, :] if ht < 11 else wb_[:, ht - 11, :]), ht == 0, ht == NHT - 1)
                              for ht in range(NHT)], reads=[b["HT"], wab, wbb], writes=[pb])
                e = ev_eng()
                fn = ACTF(Fb[:, i, cq * 256:(cq + 1) * 256], ps[:, 0:256], AF.Copy) if e == "act" else \
                    CP(Fb[:, i, cq * 256:(cq + 1) * 256], ps[:, 0:256])
                B.emit(e, fn, reads=[pb], **wr("F"))
                yield
        for i in range(nt):
            if sel_of_tile is not None:
                load_gg(sel_of_tile[i])
            tm_norm_residual(cx, i, Fb[:, i, :], [b["F"]], 1, EPS, True, out_rows=out_ap[i * 128:(i + 1) * 128, :])
            yield

    def run(g, tag=""):
        B.tag = tag
        for _ in g:
            pass

    def interleave(ga, gb, ra=1, rb=1):
        da = db = False
        while not (da and db):
            for _ in range(ra):
                if not da:
                    try:
                        next(ga)
                    except StopIteration:
                        da = True
            for _ in range(rb):
                if not db:
                    try:
                        next(gb)
                    except StopIteration:
                        db = True

    P_M1 = Pool_([6, 7])
    P_F = Pool_([0, 1, 2, 3])
    cxM1 = Cx(P_M1, UTa, ubA, 0, ybanks=[4, 5], wbs=[2, 3, 4])
    cxM2 = Cx(PALL, UTa, ubA, 12, wbs=[2, 3, 4])
    cxF = Cx(P_F, UTb, ubB, 4, wbs=[0, 1])
    cxFn = Cx(P_F, UTb, ubB, 8)

    cxPre = Cx(PALL, UTa, ubA, 0)

    def M1(src_rows, nt, segs, kind, with_y, cx=None):
        cx = cx or cxM1
        yield from prenorm_stream(cx, src_rows, nt, segs)
        yield from rnn_phase(cx, nt * 128, kind, with_y)

    def M2_prompt(bk, last_block):
        load_x(I["xm"][bk * 512:(bk + 1) * 512, :], 4)
        yield from qk_phase(cxM2, 512)
        w = load_w("w_kv", 0, cxM2)
        for i in range(4):
            last = (last_block and i == 3)
            kv_tile(cxM2, w, i * 128, 128, 1 + i, kout=(O["kp"], O["vp"]) if last else None)
        for p in range(4):
            tabs = [TAB0[:] if (bk == 0 and p == 0) else TAB[:, 0, :, :], TAB[:, 1, :, :]]
            yield from attention(cxM2, 128 * p, 128, [
                dict(kT=(lambda m, p=p: KT[:, m, 128 * p:128 * p + 128]), nk=128, va=VA[:, p, :, :], tab=tabs[0]),
                dict(kT=(lambda m, p=p: KT[:, m, 128 * p + 128:128 * p + 256]), nk=128, va=VA[:, p + 1, :, :], tab=tabs[1]),
            ])
        B.emit("act", ACTF(KT[:, :, 0:128], KT[:, :, 512:640], AF.Copy), reads=[b["KT"]], writes=[b["KT"]])
        B.emit("act", ACTF(VA[:, 0, :, 0:64], VA[:, 4, :, 0:64], AF.Copy), reads=[b["VA"]], writes=[b["VA"]])
        yield from merge_phase(cxM2, 512)
        if bk == 0:
            load_gg(0)
        yield from wout_phase(cxM2, 4, None)

    def F_pre(nt, segs):
        yield from prenorm_resident(cxF, nt, segs)

    def F_main(nt, sel, out_ap):
        cxFn.UT, cxFn.ub = cxF.UT, cxF.ub
        yield from ffn_phase_cx(nt, sel, out_ap)

    def ffn_phase_cx(nt, sel, out_ap):
        g = ffn_phase(cxF, nt, sel, out_ap)
        yield from g

    n_pre, n_main, do_sample = {2: (0, 1, False), 2.5: (4, 1, False), 3.1: (0, 4, False),
                                3.2: (0, 0, True)}.get(stage, (4, 4, True))
    PSEG = [(0, 512, 0)]
    SSEG = [(64 * s_, 64, 1 + s_) for s_ in range(4)]
    P_LO, P_HI = Pool_([0, 1, 2, 3]), Pool_([4, 5, 6, 7])

    def M2_sample(cx):
        load_x(I["xs"], 2)
        for s_ in range(4):
            load(KCT[:, s_, :, :], I["ck"][s_], [b["KCT"]], eng="pool")
            load(VAC[:, s_, :, 0:64], I["cv"][s_].rearrange("p (g d) -> p g d", g=4), [b["VAC"]], eng="pool")
            store(O["ks"][s_, 0:64, :], I["ckn"][s_, 64:128, :], [])
            store(O["vs"][s_, 0:64, :], I["cv"][s_, 64:128, :], [])
        yield from qk_phase(cx, 256)
        w = load_w("w_kv", 0, cx)
        for s_ in range(4):
            kv_tile(cx, w, 64 * s_, 64, 1 + s_, kout=(O["ks"][s_, 64:128, :], O["vs"][s_, 64:128, :]))
        for s_ in range(4):
            yield from attention(cx, 64 * s_, 64, [
                dict(kT=(lambda m, s_=s_: KCT[:, s_, m, :]), nk=128, va=VAC[:, s_, :, :], tab=TAB[:, 0, :, :]),
                dict(kT=(lambda m, s_=s_: KT[:, m, 128 + 64 * s_:128 + 64 * s_ + 64]), nk=64, va=VA[:, 1 + s_, :, :],
                     tab=TAB[:, 1, :, :]),
            ])
        yield from merge_phase(cx, 256)
        yield from wout_phase(cx, 2, [1, 2])

    def pre_block(pbk, cx):
        yield from M1(I["xp"][pbk * 512:(pbk + 1) * 512, :], 4, PSEG, "prompt", False, cx)
        if pbk == 3:
            cx2 = Cx(cx.pool, cx.UT, cx.ub, cx.ssc)
            yield from qk_phase(cx2, 512, k_only_cols=384)
            kv_tile(cx2, load_w("w_kv", 0, cx2), 384, 128, 0)

    for pbk in range(n_pre):
        ut, ub_ = (UTb, ubB) if pbk % 2 == 0 else (UTa, ubA)
        run(pre_block(pbk, Cx(P_HI, ut, ub_, 0, wbs=None)), f"pre{pbk}")
    if n_pre:
        B.emit("dve", TS_(HC[:], HC[:], FLAG[:, 0:1], None, ALU.mult), reads=[hb("HC", c_) for c_ in range(NCH)] + [b["FLAG"]], writes=[hb("HC", c_) for c_ in range(NCH)])
        B.emit("dve", TS_(CONVH[:].rearrange("p c j -> p (c j)"), CONVH[:].rearrange("p c j -> p (c j)"),
                          FLAG[:, 0:1], None, ALU.mult), reads=[hb("CONVH", c_) for c_ in range(NCH)] + [b["FLAG"]], writes=[hb("CONVH", c_) for c_ in range(NCH)])
    B.emit("pool", DMA(TAB0[:], I["tab0"]), writes=[b["TAB0"], b["GG"]], slot=cslot)
    for bk in range(n_main):
        run(M1(I["xm"][bk * 512:(bk + 1) * 512, :], 4, PSEG, "prompt", True), f"M1_{bk}")
        run(M2_prompt(bk, bk == n_main - 1), f"M2_{bk}")
        run(F_pre(4, PSEG), f"F_{bk}")
        run(F_main(4, None, I_out_m[bk]), f"F_{bk}")
    if do_sample:
        run(M1(I["xs"], 2, SSEG, "sample", True), "M1s")
        run(M2_sample(Cx(P_LO, UTa, ubA, 12)), "M2s")
        run(F_pre(2, SSEG), "Fs")
        run(F_main(2, [1, 2], O["y_s"]), "Fs")
        store(O["convs"], CONVS[:], [b["CONVS"]])
        store(O["hs"], HS[:], [b["HS"]])
    if n_main:
        store(O["convp"], CONVH[:], [hb("CONVH", c_) for c_ in range(NCH)])
        store(O["hp"], HC[:], [hb("HC", c_) for c_ in range(NCH)])
    B.barrier()
    return nc, B


def _relayout_w(w, kk=None):
    K, N = w.shape
    return np.ascontiguousarray(w.reshape(K // 128, 128, N).transpose(1, 0, 2))


def _chunks(w, cw):
    K, N = w.shape
    return np.ascontiguousarray(w.reshape(K // 128, 128, N // cw, cw).transpose(2, 1, 0, 3))


def _pp(v, n):
    return np.ascontiguousarray(v.reshape(n, 128).T)


def _tables():
    h = np.arange(1, 17, dtype=np.float32)
    slopes = np.exp2(-8.0 * h / 16).astype(np.float32)
    k = np.arange(128)
    q = np.arange(128)
    tab = np.zeros((128, 2, 16, 128), np.float32)
    for t in range(2):
        kchunk = 2 * t + k // 64 - 2
        kpos = kchunk * 64 + k % 64
        qchunk = q // 64
        qpos = q
        rel = kchunk[:, None] - qchunk[None, :]
        valid = (rel <= 0) & (rel >= -2)
        dist = np.abs(qpos[None, :] - kpos[:, None]).astype(np.float32)
        for hh in range(16):
            tab[:, t, hh, :] = np.where(valid, np.exp(-slopes[hh] * dist), 0.0)
    return tab


def prep_inputs(inp):
    f = lambda a: np.ascontiguousarray(np.asarray(a, dtype=np.float32))
    w_in = f(inp["w_in"][0])
    q_w, k_w, v_w = w_in[:, 0:1024], w_in[:, 1024:1280], w_in[:, 1280:1536]
    xr_w, yr_w = w_in[:, 1536:2816], w_in[:, 2816:4096]
    ga_w, gr_w = w_in[:, 4096:5120], w_in[:, 5120:6144]
    qcols = []
    for m in range(2):
        for j in range(4):
            for e in range(2):
                g = 2 * m + e
                qcols.append(np.arange(64) + g * 256 + j * 64)
    qcols = np.concatenate(qcols)
    xy = np.concatenate([np.concatenate([xr_w[:, c * 128:(c + 1) * 128], yr_w[:, c * 128:(c + 1) * 128]], 1)
                         for c in range(NCH)], 1)
    gg = np.concatenate([np.concatenate([ga_w[:, o * 128:(o + 1) * 128], gr_w[:, o * 128:(o + 1) * 128]], 1)
                         for o in range(8)], 1)
    gu = np.concatenate([np.concatenate([inp["w_ffn_gate"][0][:, h * 128:(h + 1) * 128],
                                         inp["w_ffn_up"][0][:, h * 128:(h + 1) * 128]], 1)
                         for h in range(NHT)], 1)
    tab = _tables()
    sel = np.zeros((5, 3, 128), np.float32)
    sel[0, 0, :] = 1.0
    for i in range(2):
        sel[1 + 2 * i, 1 + i, 0:64] = 1.0
        sel[2 + 2 * i, 1 + i, 64:128] = 1.0
    shared = {
        "w_ada": _relayout_w(f(inp["w_ada"][0])),
        "b_ada5": np.ascontiguousarray(np.broadcast_to(f(inp["b_ada"][0])[None, :], (5, 6144))),
        "w_q": _chunks(f(q_w[:, qcols]), 512), "w_k": _chunks(f(k_w), 256),
        "w_xy": _chunks(f(xy), 512), "w_kv": _chunks(f(np.concatenate([k_w, v_w], 1)), 512),
        "w_g": _chunks(f(gg), 512), "w_xr": _chunks(f(xr_w), 256),
        "w_ao": _chunks(f(inp["w_attn_o"][0]), 512), "w_ro": _chunks(f(inp["w_rnn_o"][0]), 256),
        "w_out": _chunks(f(inp["w_out"][0]), 512), "w_gu": _chunks(f(gu), 512),
        "w_dn": np.ascontiguousarray(_chunks(f(inp["w_ffn_down"][0]), 256).reshape(4, 128, 2, 11, 256).transpose(0, 2, 1, 3, 4).reshape(8, 128, 11, 256)),
        "w_rga": np.ascontiguousarray(f(inp["w_rg_a"][0]).transpose(1, 0, 2)),
        "w_rgx": np.ascontiguousarray(f(inp["w_rg_x"][0]).transpose(1, 0, 2)),
        "gpre": np.ascontiguousarray(np.stack([_pp(f(inp["g_pre_mix"][0]), 8),
                                               _pp(f(inp["g_pre_ffn"][0]), 8)], 1)),
        "gpost5": np.ascontiguousarray(np.broadcast_to(
            np.stack([f(inp["g_post_mix"][0]), f(inp["g_post_ffn"][0])], 0)[None], (5, 2, 1024))),
        "sinks": np.ascontiguousarray(np.broadcast_to(f(inp["attn_sinks"][0])[None, :], (128, 16))),
        "wconv": np.ascontiguousarray(f(inp["w_conv"][0]).reshape(4, NCH, 128).transpose(2, 1, 0)),
        "bconv": _pp(f(inp["b_conv"][0]), NCH), "ba": _pp(f(inp["b_rg_a"][0]), NCH),
        "bx": _pp(f(inp["b_rg_x"][0]), NCH), "lam": _pp(f(inp["rg_lambda"][0]), NCH),
        "tab": tab, "ident": np.eye(128, dtype=np.float32), "sel": sel,
    }
    xp_, xs_ = f(inp["x_prompt"]), f(inp["x_sample"])
    cp_, cs_ = f(inp["c_prompt"]), f(inp["c_sample"])
    ck_, cv_ = f(inp["cache_k"][0]), f(inp["cache_v"][0])
    sc_, sh_ = f(inp["state_conv"][0]), f(inp["state_h"][0])
    maps = []
    for c in range(NCORES):
        s, half = c // 2, c % 2
        cc = np.concatenate([cp_[s:s + 1], cs_[4 * c:4 * c + 4]], 0)
        cT = np.ascontiguousarray(cc.reshape(5, 8, 128).transpose(2, 1, 0))
        ck = ck_[4 * c:4 * c + 4]
        ckT = np.ascontiguousarray(ck.reshape(4, 128, 2, 2, 64).transpose(0, 3, 4, 2, 1).reshape(4, 128, 2, 128))
        cv = np.ascontiguousarray(cv_[4 * c:4 * c + 4].reshape(4, 128, 256))
        sconv = np.ascontiguousarray(sc_[4 * c:4 * c + 4].reshape(4, 3, NCH, 128).transpose(3, 2, 0, 1))
        sh0 = np.ascontiguousarray(sh_[4 * c:4 * c + 4].reshape(4, NCH, 128).transpose(2, 1, 0))
        tab0 = tab[:, 0].copy() if half == 1 else np.zeros((128, 16, 128), np.float32)
        m = dict(shared)
        m.update({
            "xm": np.ascontiguousarray(xp_[s, half * TP:(half + 1) * TP]),
            "xp": np.ascontiguousarray(xp_[s, 0:TP]),
            "xs": np.ascontiguousarray(xs_[4 * c:4 * c + 4].reshape(TS, D)),
            "cT": cT, "flag": np.full((128, 1), float(half), np.float32),
            "ck": ckT, "cv": cv, "ckn": np.ascontiguousarray(ck.reshape(4, 128, 256)), "sconv": sconv, "sh0": sh0, "tab0": tab0,
        })
        maps.append(m)
    return maps


def kernel(**inputs):
    maps = prep_inputs(inputs)
    nc, _ = build_nc()
    res = run_bass_kernel_spmd(nc, maps, core_ids=list(range(NCORES)))
    R = res.results
    y_p = np.zeros((4, 4096, D), np.float32)
    y_s = np.zeros((32, 64, D), np.float32)
    k_p = np.zeros((1, 4, 128, 4, 64), np.float32)
    v_p = np.zeros_like(k_p)
    c_p = np.zeros((1, 4, 3, 1280), np.float32)
    h_p = np.zeros((1, 4, 1280), np.float32)
    k_s = np.zeros((1, 32, 128, 4, 64), np.float32)
    v_s = np.zeros_like(k_s)
    c_s = np.zeros((1, 32, 3, 1280), np.float32)
    h_s = np.zeros((1, 32, 1280), np.float32)
    for c in range(NCORES):
        s, half = c // 2, c % 2
        r = R[c]
        y_p[s, half * TP:(half + 1) * TP] = r["y_m"]
        y_s[4 * c:4 * c + 4] = r["y_s"].reshape(4, 64, D)
        if half == 1:
            k_p[0, s] = r["kp"].reshape(128, 4, 64)
            v_p[0, s] = r["vp"].reshape(128, 4, 64)
            c_p[0, s] = r["convp"].transpose(2, 1, 0).reshape(3, 1280)
            h_p[0, s] = r["hp"].T.reshape(1280)
        k_s[0, 4 * c:4 * c + 4] = r["ks"].reshape(4, 128, 4, 64)
        v_s[0, 4 * c:4 * c + 4] = r["vs"].reshape(4, 128, 4, 64)
        c_s[0, 4 * c:4 * c + 4] = r["convs"].transpose(2, 3, 1, 0).reshape(4, 3, 1280)
        h_s[0, 4 * c:4 * c + 4] = r["hs"].transpose(2, 1, 0).reshape(4, 1280)
    return (y_p, y_s, k_p, v_p, c_p, h_p, k_s, v_s, c_s, h_s)
```

```python
import numpy as np
import concourse.bass as bass
import concourse.mybir as mybir
from concourse.bass_utils import run_bass_kernel_spmd

F32 = mybir.dt.float32
BF16 = mybir.dt.bfloat16
AF = mybir.ActivationFunctionType
ALU = mybir.AluOpType

NCORES = 8
D = 1024
KC = 8
TP = 2048
TS = 256
NCH = 10
NHT = 22
EPS = 1e-6
CHW = 4096
NHALF_ = 1
TABWIN = 0.0
SLACK = 0.0
STREAMED = ["w_xr", "w_xy", "w_q", "w_k", "w_kv", "w_ao", "w_ro", "w_g", "w_out", "w_gu", "w_dn"]
NWB = 5


class Buf:
    __slots__ = ("name", "w", "r", "excl")

    def __init__(self, name="", excl=False):
        self.name = name
        self.w = None
        self.r = {}
        self.excl = excl


class Builder:
    EPOCH = 8000

    def __init__(self, nc):
        self.nc = nc
        self.eng = {"pe": nc.tensor, "act": nc.scalar, "dve": nc.vector,
                    "pool": nc.gpsimd, "sp": nc.sync}
        self.cnt = {e: 0 for e in self.eng}
        self.esems = {e: [] for e in self.eng}
        self.seen = {e: {} for e in self.eng}
        self.slots = []
        self.nwait = 0
        self.hist = {}
        self.rec = []
        self.tags = []
        self.tag = ""
        self.sched = []
        self.recording = True
        self.sim_time = 0.0
        self.n_tab = 0

    def new_slot(self, name):
        self.slots.append({"sem": self.nc.alloc_semaphore(name), "count": 0})
        return len(self.slots) - 1

    def _esem(self, e, epoch):
        while len(self.esems[e]) <= epoch:
            self.esems[e].append(self.nc.alloc_semaphore(f"s_{e}_{len(self.esems[e])}"))
        return self.esems[e][epoch]

    def _wait(self, e, tok):
        kind, key, val = tok
        if self.seen[e].get((kind, key), -1) >= val:
            return
        self.seen[e][(kind, key)] = val
        for k, v in self.hist.get(tok, {}).items():
            if self.seen[e].get(k, -1) < v:
                self.seen[e][k] = v
        if kind == "e":
            sem = self._esem(key, val // self.EPOCH)
            v = val % self.EPOCH + 1
        else:
            sem = self.slots[key]["sem"]
            v = val
        self.eng[e].wait_ge(sem, v)
        self.nwait += 1

    def emit(self, e, fns, reads=(), writes=(), slot=None, guards=()):
        if self.recording:
            self.rec.append((e, fns if isinstance(fns, (list, tuple)) else [fns], tuple(reads), tuple(writes), slot,
                             tuple(guards)))
            self.tags.append(self.tag)
            return None
        return self._emit_now(e, fns, reads, writes, slot, guards)

    def flush(self):
        import heapq
        ops, self.rec = self.rec, []
        tags, self.tags = self.tags, []
        n = len(ops)
        lastw, readers = {}, {}
        deps = [set() for _ in range(n)]
        for i, (e, fns, reads, writes, slot, guards) in enumerate(ops):
            wl = list(writes) + [b for b in reads if b.excl]
            for b in reads:
                if id(b) in lastw:
                    deps[i].add(lastw[id(b)])
            for b in wl + list(guards):
                if id(b) in lastw:
                    deps[i].add(lastw[id(b)])
                deps[i].update(readers.get(id(b), ()))
            for b in list(reads) + list(guards):
                readers.setdefault(id(b), set()).add(i)
            for b in wl:
                lastw[id(b)] = i
                readers[id(b)] = set()
            deps[i].discard(i)
        users = [[] for _ in range(n)]
        indeg = [len(d) for d in deps]
        for i, d in enumerate(deps):
            for j in d:
                users[j].append(i)
        cost = [sum(getattr(f, "cost", 0.3) for f in op[1]) for op in ops]
        tab = [getattr(op[1][-1], "tab", None) for op in ops]
        blev = [0.0] * n
        for i in range(n - 1, -1, -1):
            m = 0.0
            for u in users[i]:
                if blev[u] > m:
                    m = blev[u]
            blev[i] = cost[i] + 0.1 + m
        fin = [0.0] * n
        ready = [0.0] * n
        rdy = {e: [] for e in self.eng}
        for i in range(n):
            if indeg[i] == 0:
                rdy[ops[i][0]].append(i)
        et = {e: 0.0 for e in self.eng}
        cur_tab = None
        slot_free = {}
        order = []
        done = 0
        while done < n:
            best = None
            for e, lst in rdy.items():
                if not lst:
                    continue
                for i in lst:
                    sl_ = ops[i][4]
                    if sl_ is not None and slot_free.get(sl_, 0.0) > ready[i]:
                        ready[i] = slot_free[sl_]
                t0 = max(et[e], min(ready[i] for i in lst)) + SLACK
                bi, bk = None, None
                win = t0 + (TABWIN if e == "act" else 0.0)
                for i in lst:
                    if ready[i] <= t0 + 1e-9 or (e == "act" and ready[i] <= win and
                                                 not (tab[i] is not None and tab[i] != cur_tab)):
                        pen = 1.3 if (e == "act" and tab[i] is not None and tab[i] != cur_tab) else 0.0
                        k = (pen > 0 and any(ready[j] <= win + 1e-9 and not (tab[j] is not None and tab[j] != cur_tab)
                                             for j in lst if j != i), -blev[i], i)
                        if e == "pool" and ops[i][4] is not None:
                            k = (False, 0.0, i)
                        if bk is None or k < bk:
                            bi, bk = i, k
                if best is None or (t0, bi) < (best[0], best[2]):
                    best = (t0, e, bi)
            t, e, i = best
            t = max(et[e], ready[i])
            rdy[e].remove(i)
            isdma = ops[i][4] is not None
            c = cost[i]
            if e == "act" and tab[i] is not None:
                if tab[i] != cur_tab:
                    c += 1.3
                    self.n_tab += 1
                cur_tab = tab[i]
            et[e] = t + (0.15 if isdma else c)
            fin[i] = t + c + 0.2
            if isdma:
                slot_free[ops[i][4]] = fin[i]
            self.sched.append((tags[i], e, t, fin[i]))
            order.append(i)
            done += 1
            for u in users[i]:
                ready[u] = max(ready[u], fin[i])
                indeg[u] -= 1
                if indeg[u] == 0:
                    rdy[ops[u][0]].append(u)
        self.sim_time = max(fin) if n else 0.0
        self.recording = False
        for i in order:
            e, fns, reads, writes, slot, guards = ops[i]
            self._emit_now(e, fns, reads, writes, slot, guards)
        self.recording = True

    def _emit_now(self, e, fns, reads=(), writes=(), slot=None, guards=()):
        deps = {}
        xr = [b for b in reads if b.excl]
        if xr:
            writes = list(writes) + xr

        def add(tok):
            if tok is None:
                return
            k = (tok[0], tok[1])
            if deps.get(k, -1) < tok[2]:
                deps[k] = tok[2]

        for b in reads:
            add(b.w)
        rset = set(id(b) for b in reads)
        for b in writes:
            if b.w is not None:
                add(b.w)
            for k, v in b.r.items():
                add((k[0], k[1], v))
        for b in guards:
            add(b.w)
            for k, v in b.r.items():
                add((k[0], k[1], v))
        if slot is not None:
            sl = self.slots[slot]
            if sl["count"] > 0:
                add(("d", slot, sl["count"] * 16))
        for (kind, key), val in deps.items():
            if kind == "e" and key == e and e == "pe":
                continue
            self._wait(e, (kind, key, val))
        if not isinstance(fns, (list, tuple)):
            fns = [fns]
        ins = None
        for f in fns:
            ins = f(self.eng[e])
        if slot is not None:
            sl["count"] += 1
            ins.then_inc(sl["sem"], 16)
            tok = ("d", slot, sl["count"] * 16)
        else:
            g = self.cnt[e]
            self.cnt[e] += 1
            ins.then_inc(self._esem(e, g // self.EPOCH), 1)
            tok = ("e", e, g)
        snap = dict(self.seen[e])
        if tok[0] == "e" and tok[2] > 0:
            snap[("e", e)] = tok[2] - 1
        self.hist[tok] = snap
        for b in list(reads) + list(guards):
            b.r[(tok[0], tok[1])] = tok[2]
        for b in writes:
            b.w = tok
            b.r = {}
        return tok

    def barrier(self):
        self.flush()
        toks = [("e", e, self.cnt[e] - 1) for e in self.eng if self.cnt[e] > 0]
        toks += [("d", i, sl["count"] * 16) for i, sl in enumerate(self.slots) if sl["count"] > 0]
        for e in self.eng:
            for t in toks:
                if t[0] == "e" and t[1] == e:
                    continue
                self._wait(e, t)


def _fd(ap):
    n = 1
    for d in ap.shape[1:]:
        n *= d
    return n


def _c(f, cost):
    f.cost = cost
    return f


def MM(out, lhsT, rhs, start, stop):
    return _c(lambda pe: pe.matmul(out, lhsT, rhs, start=start, stop=stop), max(0.06, _fd(out) / 2400.0 + 0.02))


def TR(out, in_, ident):
    return _c(lambda pe: pe.transpose(out, in_, ident), 0.1)


def ACTF(out, in_, func, bias=None, scale=1.0, accum_out=None):
    kw = {}
    if bias is not None:
        kw["bias"] = bias
    if accum_out is not None:
        kw["accum_out"] = accum_out
    f = _c(lambda a: a.activation(out, in_, func, scale=scale, **kw), _fd(out) / 1600.0 + 0.2)
    f.tab = {AF.Tanh: "E", AF.Exp: "E", AF.Sqrt: "S", AF.Silu: "L", AF.Ln: "N"}.get(func)
    return f


def TT(out, in0, in1, op):
    return _c(lambda v: v.tensor_tensor(out, in0, in1, op), _fd(out) / 1650.0 + 0.17)


def TS_(out, in0, s1, s2, op0, op1=None):
    if op1 is None:
        return _c(lambda v: v.tensor_scalar(out, in0, s1, None, op0), _fd(out) / 1650.0 + 0.17)
    return _c(lambda v: v.tensor_scalar(out, in0, s1, s2, op0, op1), _fd(out) / 1650.0 + 0.17)


def PLTS(out, in0, s1, s2, op0, op1):
    return _c(lambda v: v.tensor_scalar(out, in0, s1, s2, op0, op1), _fd(out) / 850.0 + 0.2)


def STT(out, in0, scalar, in1, op0, op1):
    return _c(lambda v: v.scalar_tensor_tensor(out, in0, scalar, in1, op0, op1), _fd(out) / 1650.0 + 0.17)


def CP(out, in_):
    return _c(lambda v: v.tensor_copy(out, in_), _fd(out) / 1650.0 + 0.17)


def DMA(out, in_):
    n = 1
    for d in out.shape:
        n *= d
    nb = n * (2 if out.dtype == BF16 else 4)
    return _c(lambda q: q.dma_start(out=out, in_=in_), 1.5 + nb / 250e3)


IN_SPECS = {
    "xm": [TP, D], "xp": [TP, D], "xs": [TS, D],
    "cT": [128, KC, 5], "flag": [128, 1],
    "ck": [4, 128, 2, 128], "cv": [4, 128, 256],
    "sconv": [128, NCH, 4, 3], "sh0": [128, NCH, 4],
    "tab0": [128, 16, 128],
    "w_ada": [128, KC, 6144], "b_ada5": [5, 6144],
    "w_q": [2, 128, KC, 512], "w_k": [1, 128, KC, 256], "w_xy": [5, 128, KC, 512],
    "w_kv": [1, 128, KC, 512], "w_g": [4, 128, KC, 512], "w_xr": [5, 128, KC, 256],
    "w_ao": [2, 128, KC, 512], "w_ro": [4, 128, NCH, 256], "w_out": [2, 128, KC, 512],
    "w_gu": [11, 128, KC, 512], "w_dn": [8, 128, 11, 256],
    "w_rga": [128, NCH, 128], "w_rgx": [128, NCH, 128],
    "gpre": [128, 2, KC], "gpost5": [5, 2, 1024],
    "sinks": [128, 16], "wconv": [128, NCH, 4], "bconv": [128, NCH],
    "ba": [128, NCH], "bx": [128, NCH], "lam": [128, NCH],
    "tab": [128, 2, 16, 128], "ident": [128, 128], "sel": [5, 3, 128],
    "ckn": [4, 128, 256],
}
OUT_SPECS = {
    "y_m": [TP, D], "y_s": [TS, D],
    "kp": [128, 256], "vp": [128, 256], "convp": [128, NCH, 3], "hp": [128, NCH],
    "ks": [4, 128, 256], "vs": [4, 128, 256], "convs": [128, NCH, 4, 3], "hs": [128, NCH, 4],
}


def build_nc(stage=99, dbg=None):
    nc = bass.Bass("TRN2", target_bir_lowering=False)
    I = {k: nc.dram_tensor(k, s, F32, kind="ExternalInput").ap() for k, s in IN_SPECS.items()}
    O = {k: nc.dram_tensor(k, s, F32, kind="ExternalOutput").ap() for k, s in OUT_SPECS.items()}
    DBG = {}
    if dbg:
        for k, s in dbg.items():
            DBG[k] = nc.dram_tensor("dbg_" + k, s, F32, kind="ExternalOutput").ap()
    B = Builder(nc)
    I_out_m = [O['y_m'][k * 512:(k + 1) * 512, :] for k in range(4)]

    def sb(name, shape, dt):
        return nc.alloc_sbuf_tensor(name, shape, dt)

    identf = sb("identf", [8, 8], F32)
    identb = sb("identb", [128, 128], BF16)
    TAB = sb("TAB", [128, 2, 16, 128], BF16)
    ESINK = sb("ESINK", [128, 16], F32)
    G1 = sb("G1", [128, 2, KC, 5], F32)
    SH = sb("SHm", [128, 2, KC, 5], F32)
    GPRE = sb("GPRE", [128, 2, KC], F32)
    GG = sb("GG", [128, 2, 1024], F32)
    TAB0 = GG[:, 0, :].bitcast(BF16)[:, 0:2048].rearrange("p (h q) -> p h q", h=16)
    WCONV = sb("WCONV", [128, NCH, 4], F32)
    BCONV = sb("BCONV", [128, NCH], F32)
    BAH = sb("BAH", [128, NCH], F32)
    BXH = sb("BXH", [128, NCH], F32)
    CCH = sb("CCH", [128, NCH], F32)
    WRGA = sb("WRGA", [128, NCH, 128], BF16)
    WRGX = sb("WRGX", [128, NCH, 128], BF16)
    HC = sb("HC", [128, NCH], F32)
    CONVH = sb("CONVH", [128, NCH, 3], F32)
    FLAG = sb("FLAG", [128, 1], F32)
    SCONV = sb("SCONV", [128, NCH, 4, 3], F32)
    SH0 = sb("SH0", [128, NCH, 4], F32)
    CONVS = sb("CONVS", [128, NCH, 4, 3], F32)
    HS = sb("HS", [128, NCH, 4], F32)
    SS = sb("SS", [128, 16], F32)
    NHALF = sb("NHALF", [128, 16], F32)
    RSTD = sb("RSTD", [128, 16], F32)
    X = sb("X", [128, 4, D], F32)
    UTa = sb("UTa", [128, KC, 512], BF16)
    UTb = sb("UTb", [128, KC, 512], BF16)
    XS = sb("XS", [128, 2, D], F32)
    XN = sb("XN", [128, 4, D], BF16)
    WB = [sb(f"WB{i}", [128, CHW], BF16) for i in range(NWB)]
    KT = sb("KT", [128, 2, 640], BF16)
    VA = sb("VA", [128, 5, 4, 66], BF16)
    KCT = sb("KCT", [128, 4, 2, 128], BF16)
    VAC = sb("VAC", [128, 4, 4, 66], BF16)
    XRb2 = [sb(f"XRb{i}", [128, 516], F32) for i in range(2)]
    XC2 = [sb(f"XC{i}", [128, 512], F32) for i in range(2)]
    XCB2 = [sb(f"XCB{i}", [128, 512], BF16) for i in range(2)]
    CT2 = [None, None]
    KVO = sb("KVO", [128, 512], F32)
    GGROW = XS[0:5, :, :]
    SEL = KVO[0:5, 0:384].rearrange("p (a n) -> p a n", a=3)
    RA2 = [sb(f"RA{i}", [128, 512], F32) for i in range(2)]
    IX2 = [sb(f"IX{i}", [128, 512], F32) for i in range(2)]
    SB2 = [sb(f"SBt{i}", [128, 512], F32) for i in range(2)]
    HH2 = [sb(f"HH{i}", [128, 512], F32) for i in range(2)]
    GT2 = [sb(f"GT{i}", [128, 512], F32) for i in range(2)]
    GT = GT2[0]
    DEN = sb("DEN", [128, 16], F32)
    REC = sb("REC", [128, 16], F32)
    ON = sb("ON", [128, D], BF16)
    TMPY = XN[:, 0:2, :].rearrange("p a d -> p (a d)").bitcast(F32)
    ARENA = sb("ARENA", [128, 25600], BF16)
    QT = ARENA[:, 0:4096].rearrange("p (k t) -> p k t", k=8)
    AOT = ARENA[:, 4096:8192].rearrange("p (k t) -> p k t", k=8)
    MT = ARENA[:, 8192:12288].rearrange("p (k t) -> p k t", k=8)
    PT = ARENA[:, 12288:20480].rearrange("p (t h q) -> p t h q", t=2, h=16)
    ROT = ARENA[:, 20480:25600].rearrange("p (k t) -> p k t", k=NCH)
    HT = ARENA[:, 0:11264].rearrange("p (k t) -> p k t", k=NHT)
    Fb = ARENA[:, 11264:19456].bitcast(F32).rearrange("p (i d) -> p i d", i=4)
    T2 = sb("T2", [128, 512], F32)

    PSW = [nc.alloc_psum_tensor(f"PSW{i}", [128, 1024], F32) for i in range(4)]
    pslots = []
    for i in range(4):
        for h in range(2):
            pslots.append((PSW[i][:, h * 512:(h + 1) * 512], Buf(f"psw{i}{h}", True)))
    st = {"wb": 0, "ev": 0, "oslot": 0, "cv": 0}

    class Pool_:
        def __init__(self, banks):
            self.banks = list(banks)
            self.i = 0
            self.pairs = [p for p in range(4) if 2 * p in self.banks and 2 * p + 1 in self.banks]
            self.pi = 0

        def ps(self):
            k = self.banks[self.i % len(self.banks)]
            self.i += 1
            return pslots[k]

        def psb(self):
            ap, buf = self.ps()
            return ap.bitcast(BF16), buf

        def psw(self):
            p = self.pairs[self.pi % len(self.pairs)]
            self.pi += 1
            return PSW[p], [pslots[2 * p][1], pslots[2 * p + 1][1]]

    PALL = Pool_(range(8))
    cur = {"pool": PALL, "UT": None, "UTb": None}

    def next_ps():
        return cur["pool"].ps()

    def next_psb():
        return cur["pool"].psb()

    def next_psw():
        return cur["pool"].psw()

    wb_bufs = [Buf(f"wb{i}") for i in range(NWB)]
    wb_slots = [B.new_slot(f"d_wb{i}") for i in range(NWB)]
    ld_slots = [B.new_slot(f"d_ld{i}") for i in range(4)]
    pl_slots = [B.new_slot(f"d_pl{i}") for i in range(4)]
    wa_slots = [B.new_slot(f"d_wa{i}") for i in range(NWB)]
    st_slots = [B.new_slot(f"d_st{i}") for i in range(4)]

    WS = {n: nc.dram_tensor("ws_" + n, IN_SPECS[n], BF16, kind="Internal").ap() for n in STREAMED}
    ws_bufs = {n: [Buf(f"ws_{n}{i}") for i in range(IN_SPECS[n][0])] for n in STREAMED}
    GGS = nc.dram_tensor("ggs", [3, 128, 2, 1024], F32, kind="Internal").ap()
    ggs_bufs = [Buf(f"ggs{i}") for i in range(3)]
    cv_slots = [B.new_slot(f"d_cv{i}") for i in range(4)]

    def convert_weights(names):
        for n in names:
            for ch in range(IN_SPECS[n][0]):
                j = st["cv"]
                st["cv"] += 1
                f = DMA(WS[n][ch], I[n][ch])
                f.cost = 2.0 * f.cost + 3.0 * (f.cost - 1.5)
                B.emit("pool", f, writes=[ws_bufs[n][ch]], slot=cv_slots[j % 2])
                yield

    def load_w(name, ch=0, cx=None, bi=None):
        if name == "w_ada":
            src3 = I["w_ada"][:, :, ch * 512:(ch + 1) * 512]
            k, n = KC, 512
        else:
            src3 = WS[name][ch]
            k, n = IN_SPECS[name][2], IN_SPECS[name][3]
        if bi is not None:
            i = bi
        elif cx is None or cx.wbs is None:
            i = st["wb"] % NWB
            st["wb"] += 1
        else:
            i = cx.wbs[cx.wbi % len(cx.wbs)]
            cx.wbi += 1
        view = WB[i][:, 0:k * n].rearrange("p (k n) -> p k n", k=k)
        if name == "w_ada":
            B.emit("pool", DMA(view, src3), writes=[wb_bufs[i]], slot=wa_slots[i])
        else:
            B.emit("sp", DMA(view, src3), reads=[ws_bufs[name][ch]], writes=[wb_bufs[i]], slot=wb_slots[i])
        return view, wb_bufs[i]

    ldc = {"n": 0}

    def load(dst, src, bufs, eng="sp"):
        s = (pl_slots if eng == "pool" else ld_slots)[ldc["n"] % 4]
        ldc["n"] += 1
        B.emit(eng, DMA(dst, src), writes=bufs, slot=s)

    def store(dst, src, bufs):
        s = st_slots[st["oslot"] % 4]
        st["oslot"] += 1
        B.emit("sp", DMA(dst, src), reads=bufs, slot=s)

    def ev_eng():
        st["ev"] += 1
        return "act" if st["ev"] % 2 else "dve"

    b = {n: Buf(n) for n in [
        "ident", "TAB", "TAB0", "ESINK", "G1", "SH", "GPRE", "GGROW", "SEL", "GG", "rgc",
        "HC", "CONVH", "FLAG", "SCONV", "SH0", "CONVS", "HS", "SS", "RSTD", "X", "UT", "XN",
        "KT", "VA", "KCT", "VAC", "KVO", "RA", "IX", "SB", "HH",
        "GT", "DEN", "REC", "ON", "TMPY", "QT", "ROT", "AOT", "MT", "PT", "T1", "T2",
        "HT", "F", "MOD", "SC", "MODT", "NHALF"]}
    xb = [Buf(f"X{i}") for i in range(4)]

    def hb(name, hf):
        k = f"{name}{hf}"
        if k not in b:
            b[k] = Buf(k)
        return b[k]


    load(identf[:], I["ident"][0:8, 0:8], [b["ident"]])
    B.emit("pool", DMA(identb[:], I["ident"]), writes=[b["ident"]], slot=B.new_slot("d_c0"))
    cslot = B.new_slot("d_c1")
    B.emit("pool", DMA(TAB[:], I["tab"]), writes=[b["TAB"]], slot=cslot)
    B.emit("pool", DMA(WRGA[:], I["w_rga"]), writes=[b["rgc"]], slot=cslot)
    B.emit("pool", DMA(WRGX[:], I["w_rgx"]), writes=[b["rgc"]], slot=cslot)
    load(ESINK[:], I["sinks"], [b["ESINK"]])
    load(GPRE[:], I["gpre"], [b["GPRE"]])
    load(SEL[:], I["sel"], [b["SEL"]])
    load(WCONV[:], I["wconv"], [b["rgc"]])
    load(BCONV[:], I["bconv"], [b["rgc"]])
    load(BAH[:], I["ba"], [b["rgc"]])
    load(BXH[:], I["bx"], [b["rgc"]])
    load(CCH[:], I["lam"], [b["rgc"]])
    load(FLAG[:], I["flag"], [b["FLAG"]])
    load(SCONV[:], I["sconv"], [b["SCONV"]])
    load(SH0[:], I["sh0"], [b["SH0"]])
    MOD = ARENA[0:5, 0:12288].bitcast(F32)
    BADA = ARENA[0:5, 12288:24576].bitcast(F32)
    GPOST = XN[0:5, :, :].rearrange("p a d -> p (a d)").bitcast(F32).rearrange("p (a d) -> p a d", a=2)
    CT = T2[:, 160:200].rearrange("p (k s) -> p k s", k=KC)
    SC = T2[:, 200:220].bitcast(BF16).rearrange("p (k s) -> p k s", k=KC)
    MODT = T2[:, 0:160].rearrange("p (w k s) -> p w k s", w=4, k=KC)
    load(BADA[:], I["b_ada5"], [b["MOD"]])
    load(GPOST[:], I["gpost5"], [b["GGROW"]])
    load(CT[:], I["cT"], [b["SC"]])

    if stage == 0.1:
        B.barrier()
        return nc, B
    B.emit("act", ACTF(ESINK[:], ESINK[:], AF.Exp), reads=[b["ESINK"]], writes=[b["ESINK"]])
    B.emit("act", ACTF(CCH[:], CCH[:], AF.Exp, scale=-1.0), reads=[b["rgc"]], writes=[b["rgc"]])
    B.emit("act", ACTF(CCH[:], CCH[:], AF.Ln, bias=1.0), reads=[b["rgc"]], writes=[b["rgc"]])
    B.emit("dve", TS_(CCH[:], CCH[:], -4.0, None, ALU.mult), reads=[b["rgc"]], writes=[b["rgc"]])
    B.emit("dve", TS_(BAH[:], BAH[:], 0.5, None, ALU.mult), reads=[b["rgc"]], writes=[b["rgc"]])
    B.emit("dve", TS_(BXH[:], BXH[:], 0.5, None, ALU.mult), reads=[b["rgc"]], writes=[b["rgc"]])

    if stage == 0.2:
        B.barrier()
        return nc, B
    B.emit("act", ACTF(SC[:], CT[:], AF.Silu), reads=[b["SC"]], writes=[b["SC"]])
    for j in range(12):
        wv, wbuf = load_w("w_ada", j)
        ps, pb = next_ps()
        B.emit("pe", [MM(ps[0:5, :], SC[:, kc, :], wv[:, kc, :], kc == 0, kc == KC - 1)
                      for kc in range(KC)], reads=[b["SC"], wbuf], writes=[pb])
        B.emit("dve", TT(MOD[:, j * 512:(j + 1) * 512], ps[0:5, :], BADA[:, j * 512:(j + 1) * 512],
                         ALU.add), reads=[pb, b["MOD"]], writes=[b["MOD"]])
    if stage == 0.3:
        B.barrier()
        return nc, B
    ps, pb = next_ps()
    fns = []
    for w, col in enumerate([0, 1, 3, 4]):
        for kc in range(KC):
            o = (w * KC + kc) * 5
            fns.append(TR(ps[:, o:o + 5], MOD[0:5, col * 1024 + kc * 128: col * 1024 + (kc + 1) * 128],
                          identf[0:5, 0:5]))
    B.emit("pe", fns, reads=[b["MOD"], b["ident"]], writes=[pb])
    B.emit("dve", CP(MODT[:].rearrange("p w k s -> p (w k s)"), ps[:, 0:160]),
           reads=[pb], writes=[b["MODT"]])
    for mf in range(2):
        for s in range(5):
            B.emit("dve", STT(G1[:, mf, :, s], MODT[:, 2 * mf + 1, :, s], 1.0, GPRE[:, mf, :],
                              ALU.add, ALU.mult),
                   reads=[b["MODT"], b["GPRE"]], writes=[b["G1"]])
        B.emit("dve", CP(SH[:, mf, :, :], MODT[:, 2 * mf, :, :]), reads=[b["MODT"]], writes=[b["SH"]])
        B.emit("dve", TT(GGROW[:, mf, :], MOD[:, (2 + 3 * mf) * 1024:(3 + 3 * mf) * 1024],
                         GPOST[:, mf, :], ALU.mult), reads=[b["MOD"], b["GGROW"]], writes=[b["GGROW"]])

    if stage == 0.4:
        B.barrier()
        return nc, B

    class Cx:
        def __init__(self, pool, UT, ub, ssc, ybanks=None, wbs=None):
            self.pool, self.UT, self.ub, self.ssc, self.ybanks = pool, UT, ub, ssc, ybanks
            self.wbs, self.wbi = wbs, 0

    ubA, ubB = Buf("UTa"), Buf("UTb")
    ssb = [Buf(f"ss{i}") for i in range(16)]

    for g_ in range(4):
        b[f"PT{g_}"] = Buf(f"PT{g_}")
    PTS = [b[f"PT{g_}"] for g_ in range(4)]

    def wr(name):
        al = {"QT": ["HT"], "AOT": ["HT"], "MT": ["HT", "F"], "HT": ["QT", "AOT", "MT"],
              "F": ["MT"] + [f"PT{g_}" for g_ in range(4)]}
        for g_ in range(4):
            al[f"PT{g_}"] = ["F"]
        return dict(writes=[b[name]], guards=[b[a] for a in al.get(name, [])])

    def build_gg(cx, selidx):
        for mf in range(2):
            for h in range(2):
                ps, pb = cx.pool.ps()
                B.emit("pe", MM(ps[:, :], SEL[:, selidx, :], GGROW[:, mf, h * 512:(h + 1) * 512], True, True),
                       reads=[b["SEL"], b["GGROW"]], writes=[pb])
                B.emit("dve", CP(GG[:, mf, h * 512:(h + 1) * 512], ps[:, :]), reads=[pb], writes=[b["GG"], b["TAB0"]])

    for v_ in range(3):
        build_gg(Cx(PALL, None, None, 0), v_)
        B.emit("sp", DMA(GGS[v_], GG[:]), reads=[b["GG"]], writes=[ggs_bufs[v_]], slot=st_slots[v_])

    def load_gg(v_):
        s_ = ld_slots[ldc["n"] % 4]
        ldc["n"] += 1
        B.emit("sp", DMA(GG[:], GGS[v_]), reads=[ggs_bufs[v_]], writes=[b["GG"], b["TAB0"]], slot=s_)

    def rstd_from_ss(c0, nt, eps, sbuf=None):
        sb_ = ssb[c0:c0 + nt]
        B.emit("pool", TS_(RSTD[:, c0:c0 + nt], SS[:, c0:c0 + nt], 1.0 / D, eps, ALU.mult, ALU.add),
               reads=sb_, writes=sb_)
        B.emit("pool", TT(RSTD[:, c0:c0 + nt], RSTD[:, c0:c0 + nt], NHALF[:, 0:nt], ALU.pow),
               reads=sb_ + [b["NHALF"]], writes=sb_)

    xsb = [Buf("XS0"), Buf("XS1")]
    xnb = [Buf(f"XN{i}") for i in range(4)]
    b["KVO"] = Buf("KVO")

    def transposes_to_UT(cx, nt, mf, segs):
        for kc in range(KC):
            ps, pb = cx.pool.psb()
            B.emit("pe", [TR(ps[:, i * 128:(i + 1) * 128], XN[:, i, kc * 128:(kc + 1) * 128], identb[:])
                          for i in range(nt)], reads=xnb[0:nt] + [b["ident"]], writes=[pb])
            for (c0, n, s) in segs:
                e = ev_eng()
                if e == "act":
                    B.emit("act", ACTF(cx.UT[:, kc, c0:c0 + n], ps[:, c0:c0 + n], AF.Identity,
                                       bias=SH[:, mf, kc, s:s + 1], scale=G1[:, mf, kc, s:s + 1]),
                           reads=[pb, b["G1"], b["SH"]], writes=[cx.ub])
                else:
                    B.emit("dve", TS_(cx.UT[:, kc, c0:c0 + n], ps[:, c0:c0 + n], G1[:, mf, kc, s:s + 1],
                                      SH[:, mf, kc, s:s + 1], ALU.mult, ALU.add),
                           reads=[pb, b["G1"], b["SH"]], writes=[cx.ub])
            yield

    def prenorm_stream(cx, src_rows, nt, segs):
        c0 = cx.ssc
        for i in range(nt):
            load(XS[:, i % 2, :], src_rows[i * 128:(i + 1) * 128, :], [xsb[i % 2]])
            B.emit("act", ACTF(XN[:, i, :], XS[:, i % 2, :], AF.Square, accum_out=SS[:, c0 + i:c0 + i + 1]),
                   reads=[xsb[i % 2]], writes=[xnb[i], ssb[c0 + i]])
            yield
        rstd_from_ss(c0, nt, EPS)
        for i in range(nt):
            load(XS[:, i % 2, :], src_rows[i * 128:(i + 1) * 128, :], [xsb[i % 2]])
            B.emit("pool", PLTS(XN[:, i, :], XS[:, i % 2, :], RSTD[:, c0 + i:c0 + i + 1], 0.0, ALU.mult, ALU.add),
                   reads=[xsb[i % 2], ssb[c0 + i]], writes=[xnb[i]])
            yield
        yield from transposes_to_UT(cx, nt, 0, segs)

    def prenorm_resident(cx, nt, segs):
        c0 = cx.ssc
        for i in range(nt):
            B.emit("act", ACTF(XN[:, i, :], X[:, i, :], AF.Square, accum_out=SS[:, c0 + i:c0 + i + 1]),
                   reads=[xb[i]], writes=[xnb[i], ssb[c0 + i]])
        rstd_from_ss(c0, nt, EPS)
        for i in range(nt):
            B.emit("pool", PLTS(XN[:, i, :], X[:, i, :], RSTD[:, c0 + i:c0 + i + 1], 0.0, ALU.mult, ALU.add),
                   reads=[xb[i], ssb[c0 + i]], writes=[xnb[i]])
        yield from transposes_to_UT(cx, nt, 1, segs)

    def load_x(src_rows, nt):
        for i in range(nt):
            load(X[:, i, :], src_rows[i * 128:(i + 1) * 128, :], [xb[i]])

    B.emit("dve", lambda v: v.memset(VA[:, :, :, 64:66], 1.0), writes=[b["VA"]])
    B.emit("dve", lambda v: v.memset(VAC[:, :, :, 64:66], 1.0), writes=[b["VAC"]])
    B.emit("dve", lambda v: v.memset(HC[:], 0.0), writes=[hb("HC", c_) for c_ in range(NCH)])
    B.emit("dve", lambda v: v.memset(NHALF[:], -0.5), writes=[b["NHALF"]])
    B.emit("dve", lambda v: v.memset(CONVH[:], 0.0), writes=[hb("CONVH", c_) for c_ in range(NCH)])
    B.barrier()
    for _ in convert_weights(STREAMED):
        pass

    def v3(ap, a):
        return ap.rearrange("p (a n) -> p a n", a=a)

    def qk_phase(cx, T, k_only_cols=None):
        if k_only_cols is None:
            for ch in range(2):
                wv, wbuf = load_w("w_q", ch, cx)
                for tl in range(4):
                    ps, pb = cx.pool.ps()
                    B.emit("pe", [MM(ps[:, 0:T], wv[:, kc, tl * 128:(tl + 1) * 128], cx.UT[:, kc, 0:T],
                                     kc == 0, kc == KC - 1) for kc in range(KC)],
                           reads=[cx.ub, wbuf], writes=[pb])
                    e = ev_eng()
                    fn = ACTF(QT[:, ch * 4 + tl, 0:T], ps[:, 0:T], AF.Copy) if e == "act" else \
                        CP(QT[:, ch * 4 + tl, 0:T], ps[:, 0:T])
                    B.emit(e, fn, reads=[pb], **wr("QT"))
                    yield
        wv, wbuf = load_w("w_k", 0, cx)
        for m in range(2):
            ps, pb = cx.pool.ps()
            B.emit("pe", [MM(ps[:, 0:T], wv[:, kc, m * 128:(m + 1) * 128], cx.UT[:, kc, 0:T],
                             kc == 0, kc == KC - 1) for kc in range(KC)],
                   reads=[cx.ub, wbuf], writes=[pb])
            if k_only_cols is None:
                B.emit("act", ACTF(KT[:, m, 128:128 + T], ps[:, 0:T], AF.Copy), reads=[pb], writes=[b["KT"]])
            else:
                c0 = k_only_cols
                B.emit("act", ACTF(KT[:, m, 0:128], ps[:, c0:c0 + 128], AF.Copy), reads=[pb], writes=[b["KT"]])
            yield

    def kv_tile(cx, w, tok0, ntok, va_tile, kout=None):
        wv, wbuf = w
        ps, pb = cx.pool.ps()
        B.emit("pe", [MM(ps[0:ntok, :], cx.UT[:, kc, tok0:tok0 + ntok], wv[:, kc, :], kc == 0, kc == KC - 1)
                      for kc in range(KC)], reads=[cx.ub, wbuf], writes=[pb])
        B.emit("act", ACTF(VA[0:ntok, va_tile, :, 0:64], v3(ps[0:ntok, 256:512], 4), AF.Copy),
               reads=[pb], writes=[b["VA"]])
        if kout is not None:
            B.emit("dve", CP(KVO[0:ntok, :], ps[0:ntok, :]), reads=[pb], writes=[b["KVO"]])
            store(kout[0], KVO[0:ntok, 0:256], [b["KVO"]])
            store(kout[1], KVO[0:ntok, 256:512], [b["KVO"]])

    def rnn_channel(cx, c, T, kind, ps_x, pbx, ps_y, pby):
        sample = (kind == "sample")
        H = T // NHALF_
        st_ = c % 2
        XRb, XC, XCB, CT = XRb2[st_], XC2[st_], XCB2[st_], CT2[st_]
        RA, IX, SB_, HH, GT = RA2[st_], IX2[st_], SB2[st_], HH2[st_], GT2[st_]
        xrB = hb("XRb", st_)
        if sample:
            XR3 = v3(XRb[:, 0:268], 4)
            B.emit("dve", CP(XR3[:, :, 0:3], SCONV[:, c, :, :]), reads=[b["SCONV"]], writes=[xrB])
            B.emit("act", ACTF(XR3[:, :, 3:67], v3(ps_x[:, 0:256], 4), AF.Copy), reads=[pbx], writes=[xrB])
            B.emit("dve", CP(CONVS[:, c, :, :], XR3[:, :, 64:67]), reads=[xrB], writes=[b["CONVS"]])
        else:
            B.emit("dve", CP(XRb[:, 0:3], CONVH[:, c, :]), reads=[hb("CONVH", c)], writes=[xrB])
            B.emit("act", ACTF(XRb[:, 3:3 + T], ps_x[:, 0:T], AF.Copy), reads=[pbx], writes=[xrB])
            B.emit("dve", CP(CONVH[:, c, :], XRb[:, T:T + 3]), reads=[xrB], writes=[hb("CONVH", c)])
        yield

        def half(hf):
            lo, hi = hf * H, (hf + 1) * H
            cs = slice(lo, hi)
            if sample:
                XR3 = v3(XRb[:, 0:268], 4)
                taps = [XR3[:, (4 // NHALF_) * hf:(4 // NHALF_) * (hf + 1), j:j + 64] for j in range(4)]
                xc_o, tmp_o = v3(XC[:, cs], 4 // NHALF_), None
            else:
                taps = [XRb[:, lo + j:hi + j] for j in range(4)]
                xc_o, tmp_o = XC[:, cs], None
            SBb, RAb, IXb, GTb = (hb(f"{n}s{st_}", hf) for n in ("SB", "RA", "IX", "GT"))
            XCb, XCBb, CTb = (hb(f"{n}s{st_}", hf) for n in ("XC", "XCB", "CT"))
            if False:
                B.emit("pool", TS_(xc_o, taps[3], WCONV[:, c, 3:4], BCONV[:, c:c + 1], ALU.mult, ALU.add),
                       reads=[xrB, b["rgc"]], writes=[XCb])
                for j in (2, 1, 0):
                    B.emit("pool", TS_(tmp_o, taps[j], WCONV[:, c, j:j + 1], 0.0, ALU.mult, ALU.add),
                           reads=[xrB, b["rgc"]], writes=[CTb])
                    B.emit("pool", TT(xc_o, xc_o, tmp_o, ALU.add), reads=[XCb, CTb], writes=[XCb])
            else:
                B.emit("pool", PLTS(xc_o, taps[3], WCONV[:, c, 3:4], BCONV[:, c:c + 1], ALU.mult, ALU.add),
                       reads=[xrB, b["rgc"]], writes=[XCb])
                for j in (2, 1, 0):
                    B.emit("dve", STT(xc_o, taps[j], WCONV[:, c, j:j + 1], xc_o, ALU.mult, ALU.add),
                           reads=[xrB, XCb, b["rgc"]], writes=[XCb])
            yield
            B.emit("dve", CP(XCB[:, cs], XC[:, cs]), reads=[XCb], writes=[XCBb])
            pg, pgb = cx.pool.ps()
            if NHALF_ == 2:
                pg2, pgb2, o2 = pg, pgb, 256
                B.emit("pe", [MM(pg[:, 0:H], WRGA[:, c, :], XCB[:, cs], True, True),
                              MM(pg[:, 256:256 + H], WRGX[:, c, :], XCB[:, cs], True, True)],
                       reads=[XCBb, b["rgc"]], writes=[pgb])
            else:
                pg2, pgb2 = cx.pool.ps()
                o2 = 0
                B.emit("pe", MM(pg[:, 0:H], WRGA[:, c, :], XCB[:, cs], True, True), reads=[XCBb, b["rgc"]], writes=[pgb])
                B.emit("pe", MM(pg2[:, 0:H], WRGX[:, c, :], XCB[:, cs], True, True), reads=[XCBb, b["rgc"]], writes=[pgb2])
            yield
            B.emit("act", ACTF(RA[:, cs], pg[:, 0:H], AF.Tanh, bias=BAH[:, c:c + 1], scale=0.5),
                   reads=[pgb, b["rgc"]], writes=[RAb])
            B.emit("act", ACTF(IX[:, cs], pg2[:, o2:o2 + H], AF.Tanh, bias=BXH[:, c:c + 1], scale=0.5),
                   reads=[pgb2, b["rgc"]], writes=[IXb])
            yield
            B.emit("act", ACTF(RA[:, cs], RA[:, cs], AF.Exp, bias=CCH[:, c:c + 1], scale=CCH[:, c:c + 1]),
                   reads=[RAb, b["rgc"]], writes=[RAb])
            B.emit("dve", STT(IX[:, cs], IX[:, cs], 1.0, XC[:, cs], ALU.add, ALU.mult), reads=[IXb, XCb], writes=[IXb])
            yield
            B.emit("dve", TT(SB_[:, cs], RA[:, cs], RA[:, cs], ALU.mult), reads=[RAb], writes=[SBb])
            if ps_y is not None:
                B.emit("act", ACTF(GT[:, cs], ps_y[:, cs], AF.Square), reads=[pby], writes=[GTb])
                B.emit("pool", PLTS(GT[:, cs], GT[:, cs], 0.044715, 1.0, ALU.mult, ALU.add), reads=[GTb], writes=[GTb])
                B.emit("dve", TT(GT[:, cs], GT[:, cs], ps_y[:, cs], ALU.mult), reads=[GTb, pby], writes=[GTb])
            yield
            if ps_y is not None:
                B.emit("act", ACTF(GT[:, cs], GT[:, cs], AF.Tanh, scale=0.7978845608028654), reads=[GTb], writes=[GTb])
                yield
                B.emit("dve", STT(GT[:, cs], GT[:, cs], 1.0, ps_y[:, cs], ALU.add, ALU.mult), reads=[GTb, pby], writes=[GTb])
            yield

        gens = [half(hf) for hf in range(NHALF_)]
        while gens:
            for g_ in list(gens):
                try:
                    next(g_)
                except StopIteration:
                    gens.remove(g_)
            yield
        for hf in range(NHALF_):
            cs = slice(hf * H, (hf + 1) * H)
            B.emit("act", ACTF(SB_[:, cs], SB_[:, cs], AF.Sqrt, bias=1.0, scale=-1.0), reads=[hb(f"SBs{st_}", hf)], writes=[hb(f"SBs{st_}", hf)])
        for hf in range(NHALF_):
            cs = slice(hf * H, (hf + 1) * H)
            B.emit("dve", STT(SB_[:, cs], SB_[:, cs], 0.5, IX[:, cs], ALU.mult, ALU.mult),
                   reads=[hb(f"SBs{st_}", hf), hb(f"IXs{st_}", hf)], writes=[hb(f"SBs{st_}", hf)])
        allab = [hb(f"RAs{st_}", h_) for h_ in range(NHALF_)] + [hb(f"SBs{st_}", h_) for h_ in range(NHALF_)]
        if sample:
            for s_ in range(4):
                B.emit("dve", lambda v, s_=s_: v.tensor_tensor_scan(
                    HH[:, s_ * 64:(s_ + 1) * 64], RA[:, s_ * 64:(s_ + 1) * 64], SB_[:, s_ * 64:(s_ + 1) * 64],
                    SH0[:, c, s_:s_ + 1], ALU.mult, ALU.add),
                    reads=allab + [b["SH0"]], writes=[hb("HHs", st_)])
            B.emit("dve", CP(HS[:, c, :], v3(HH[:, 0:256], 4)[:, :, 63]), reads=[hb("HHs", st_)], writes=[b["HS"]])
        else:
            B.emit("dve", lambda v: v.tensor_tensor_scan(HH[:, 0:T], RA[:, 0:T], SB_[:, 0:T], HC[:, c:c + 1],
                                                         ALU.mult, ALU.add),
                   reads=allab + [hb("HC", c)], writes=[hb("HHs", st_)])
            B.emit("dve", CP(HC[:, c:c + 1], HH[:, T - 1:T]), reads=[hb("HHs", st_)], writes=[hb("HC", c)])
        yield
        if ps_y is None:
            return
        B.emit("dve", STT(ROT[:, c, 0:T], GT[:, 0:T], 0.5, HH[:, 0:T], ALU.mult, ALU.mult),
               reads=[hb(f"GTs{st_}", h_) for h_ in range(NHALF_)] + [hb("HHs", st_)], writes=[b["ROT"]])
        yield

    def rnn_phase(cx, T, kind, with_y):
        active = []

        def step_all():
            for a_ in list(active):
                try:
                    next(a_[0])
                    a_[1] += 1
                except StopIteration:
                    active.remove(a_)

        for c in range(NCH):
            while len(active) >= 2 or (active and active[0][1] < 6):
                step_all()
                yield
            if with_y:
                if c % 2 == 0:
                    wv, wbuf = load_w("w_xy", c // 2, cx)
                cc = c % 2
                xcols, ycols = slice((2 * cc) * 128, (2 * cc + 1) * 128), slice((2 * cc + 1) * 128, (2 * cc + 2) * 128)
            else:
                if c % 2 == 0:
                    wv, wbuf = load_w("w_xr", c // 2, cx)
                xcols = slice((c % 2) * 128, (c % 2 + 1) * 128)
            ps_x, pbx = cx.pool.ps()
            B.emit("pe", [MM(ps_x[:, 0:T], wv[:, kc, xcols], cx.UT[:, kc, 0:T], kc == 0, kc == KC - 1)
                          for kc in range(KC)], reads=[cx.ub, wbuf], writes=[pbx])
            ps_y = pby = None
            if with_y:
                ps_y, pby = pslots[cx.ybanks[c % 2]] if cx.ybanks else cx.pool.ps()
                B.emit("pe", [MM(ps_y[:, 0:T], wv[:, kc, ycols], cx.UT[:, kc, 0:T], kc == 0, kc == KC - 1)
                              for kc in range(KC)], reads=[cx.ub, wbuf], writes=[pby])
            active.append([rnn_channel(cx, c, T, kind, ps_x, pbx, ps_y, pby), 0])
        while active:
            step_all()
            yield

    def attn_scores(cx, g, q0, nq, ktiles):
        m, e = g // 2, g % 2
        prt = slice(64 * e, 64 * e + 64)
        for t, kt in enumerate(ktiles):
            nk = kt["nk"]
            S, sbuf_ = cx.pool.ps()
            B.emit("pe", MM(v3(S[0:nk, 0:4 * nq], 4), kt["kT"](m)[prt, :], QT[prt, 4 * m:4 * m + 4, q0:q0 + nq],
                            True, True), reads=[b["KT"], b["KCT"], b["QT"]], writes=[sbuf_])
            B.emit("act", ACTF(PT[0:nk, t, 4 * g:4 * g + 4, 0:nq], v3(S[0:nk, 0:4 * nq], 4), AF.Exp, scale=0.125),
                   reads=[sbuf_], **wr(f"PT{g}"))
            B.emit("dve", TT(PT[0:nk, t, 4 * g:4 * g + 4, 0:nq], PT[0:nk, t, 4 * g:4 * g + 4, 0:nq],
                             kt["tab"][0:nk, 4 * g:4 * g + 4, 0:nq], ALU.mult),
                   reads=[b[f"PT{g}"], b["TAB"], b["TAB0"]], **wr(f"PT{g}"))

    def attn_pv(cx, g, nq, ktiles):
        Ob, obuf = cx.pool.ps()
        fns = []
        for j in range(4):
            h = 4 * g + j
            for t, kt in enumerate(ktiles):
                nk = kt["nk"]
                fns.append(MM(Ob[0:nq, 65 * j:65 * j + 65], PT[0:nk, t, h, 0:nq], kt["va"][0:nk, g, 0:65],
                              t == 0, t == len(ktiles) - 1))
        B.emit("pe", fns, reads=[b[f"PT{g}"], b["VA"], b["VAC"]], writes=[obuf])
        O3 = Ob[0:nq, 0:260].rearrange("p (h e) -> p h e", e=65)
        B.emit("dve", TT(DEN[0:nq, 4 * g:4 * g + 4], O3[:, :, 64], ESINK[0:nq, 4 * g:4 * g + 4], ALU.add),
               reads=[obuf, b["ESINK"]], writes=[hb("DEN", g)])
        B.emit("dve", lambda v, g=g: v.reciprocal(REC[0:nq, 4 * g:4 * g + 4], DEN[0:nq, 4 * g:4 * g + 4]),
               reads=[hb("DEN", g)], writes=[hb("REC", g)])
        for j in range(4):
            h = 4 * g + j
            B.emit("dve", TS_(ON[0:nq, 64 * h:64 * h + 64], Ob[0:nq, 65 * j:65 * j + 64], REC[0:nq, h:h + 1],
                              None, ALU.mult), reads=[obuf, hb("REC", g)], writes=[b["ON"]])

    def attention(cx, q0, nq, ktiles):
        attn_scores(cx, 0, q0, nq, ktiles)
        for g in range(4):
            if g + 1 < 4:
                attn_scores(cx, g + 1, q0, nq, ktiles)
            attn_pv(cx, g, nq, ktiles)
            yield
        psb, pbb = cx.pool.psb()
        B.emit("pe", [TR(psb[:, kc * 128:kc * 128 + nq], ON[0:nq, kc * 128:(kc + 1) * 128], identb[0:nq, 0:nq])
                      for kc in range(KC)], reads=[b["ON"], b["ident"]], writes=[pbb])
        B.emit("act", ACTF(AOT[:, :, q0:q0 + nq], v3(psb[:, :], 8)[:, :, 0:nq], AF.Copy), reads=[pbb], **wr("AOT"))
        yield

    def merge_phase(cx, T):
        wao = wro = wg = None
        for ot in range(8):
            if ot % 4 == 0:
                mb = cx.wbs if cx.wbs is not None else [None, None, None]
                wao = load_w("w_ao", ot // 4, cx, bi=mb[0])
            if ot % 2 == 0:
                wro = load_w("w_ro", ot // 2, cx, bi=mb[1])
                wg = load_w("w_g", ot // 2, cx, bi=mb[2])
            o4, o2 = ot % 4, ot % 2
            pGa, bGa = cx.pool.ps()
            B.emit("pe", [MM(pGa[:, 0:T], wg[0][:, kc, (2 * o2) * 128:(2 * o2 + 1) * 128], cx.UT[:, kc, 0:T], kc == 0, kc == KC - 1)
                          for kc in range(KC)], reads=[cx.ub, wg[1]], writes=[bGa])
            pGr, bGr = cx.pool.ps()
            B.emit("pe", [MM(pGr[:, 0:T], wg[0][:, kc, (2 * o2 + 1) * 128:(2 * o2 + 2) * 128], cx.UT[:, kc, 0:T], kc == 0, kc == KC - 1)
                          for kc in range(KC)], reads=[cx.ub, wg[1]], writes=[bGr])
            pA, bA = cx.pool.ps()
            B.emit("pe", [MM(pA[:, 0:T], wao[0][:, kc, o4 * 128:(o4 + 1) * 128], AOT[:, kc, 0:T], kc == 0, kc == KC - 1)
                          for kc in range(KC)], reads=[b["AOT"], wao[1]], writes=[bA])
            pR, bR = cx.pool.ps()
            B.emit("pe", [MM(pR[:, 0:T], wro[0][:, c, o2 * 128:(o2 + 1) * 128], ROT[:, c, 0:T], c == 0, c == NCH - 1)
                          for c in range(NCH)], reads=[b["ROT"], wro[1]], writes=[bR])
            T1h, T2h = ON[:, :].bitcast(F32)[:, 0:T], T2[:, 0:T]
            t1b = [b["ON"]]
            B.emit("act", ACTF(T1h, pGa[:, 0:T], AF.Tanh, scale=0.5), reads=[bGa], writes=t1b)
            B.emit("act", ACTF(T2h, pGr[:, 0:T], AF.Tanh, scale=0.5), reads=[bGr], writes=[b["T2"]])
            B.emit("dve", STT(T1h, T1h, 1.0, pA[:, 0:T], ALU.add, ALU.mult), reads=t1b + [bA], writes=t1b)
            B.emit("dve", STT(T2h, T2h, 1.0, pR[:, 0:T], ALU.add, ALU.mult), reads=[b["T2"], bR], writes=[b["T2"]])
            B.emit("dve", TT(MT[:, ot, 0:T], T1h, T2h, ALU.add), reads=t1b + [b["T2"]], **wr("MT"))
            yield

    def tm_norm_residual(cx, i, src, src_bufs, mf, eps, inplace, out_rows=None):
        c = cx.ssc + i
        sbuf = ssb[c]
        B.emit("act", ACTF(ON[:, :], src, AF.Square, accum_out=SS[:, c:c + 1]), reads=src_bufs, writes=[b["ON"], sbuf])
        rstd_from_ss(c, 1, eps, sbuf)
        if inplace:
            dst, dbufs = src, src_bufs
        else:
            dst, dbufs = Fb[:, 1, :], [b["F"]]
        if inplace:
            B.emit("dve", STT(dst, src, RSTD[:, c:c + 1], GG[:, mf, :], ALU.mult, ALU.mult),
                   reads=src_bufs + [sbuf, b["GG"]], writes=dbufs)
        else:
            B.emit("dve", STT(dst, src, RSTD[:, c:c + 1], GG[:, mf, :], ALU.mult, ALU.mult),
                   reads=src_bufs + [sbuf, b["GG"]], **wr("F"))
        B.emit("dve", TT(X[:, i, :], dst, X[:, i, :], ALU.add), reads=dbufs + [xb[i]], writes=[xb[i]])
        if out_rows is not None:
            store(out_rows, X[:, i, :], [xb[i]])

    def wout_phase(cx, nt, sel_of_tile):
        w0 = load_w("w_out", 0, cx)
        w1 = load_w("w_out", 1, cx)
        for i in range(nt):
            if sel_of_tile is not None:
                load_gg(sel_of_tile[i])
            pw, pwb = cx.pool.psw()
            for h, w in enumerate((w0, w1)):
                B.emit("pe", [MM(pw[:, h * 512:(h + 1) * 512], MT[:, kc, i * 128:(i + 1) * 128], w[0][:, kc, :],
                                 kc == 0, kc == KC - 1) for kc in range(KC)], reads=[b["MT"], w[1]], writes=[pwb[h]])
            tm_norm_residual(cx, i, pw[:, :], list(pwb), 0, 4.0 * EPS, False)
            yield

    def ffn_phase(cx, nt, sel_of_tile, out_ap):
        T = nt * 128
        for ch in range(11):
            wv, wbuf = load_w("w_gu", ch, cx)
            for cc in range(2):
                ht = 2 * ch + cc
                pG, bG = cx.pool.ps()
                B.emit("pe", [MM(pG[:, 0:T], wv[:, kc, (2 * cc) * 128:(2 * cc + 1) * 128], cx.UT[:, kc, 0:T], kc == 0, kc == KC - 1)
                              for kc in range(KC)], reads=[cx.ub, wbuf], writes=[bG])
                yield
                pU, bU = cx.pool.ps()
                B.emit("pe", [MM(pU[:, 0:T], wv[:, kc, (2 * cc + 1) * 128:(2 * cc + 2) * 128], cx.UT[:, kc, 0:T], kc == 0, kc == KC - 1)
                              for kc in range(KC)], reads=[cx.ub, wbuf], writes=[bU])
                if ht % 2 == 0:
                    Tt, tb = T2[:, 0:T], b["T2"]
                else:
                    Tt, tb = ON[:, :].bitcast(F32)[:, 0:T], b["ON"]
                B.emit("act", ACTF(Tt, pG[:, 0:T], AF.Tanh, scale=0.5), reads=[bG], writes=[tb])
                B.emit("dve", STT(Tt, Tt, 1.0, pG[:, 0:T], ALU.add, ALU.mult), reads=[tb, bG], writes=[tb])
                B.emit("dve", STT(HT[:, ht, 0:T], Tt, 0.5, pU[:, 0:T], ALU.mult, ALU.mult), reads=[tb, bU], **wr("HT"))
                yield
        for cq in range(4):
            wa, wab = load_w("w_dn", 2 * cq, cx)
            wb_, wbb = load_w("w_dn", 2 * cq + 1, cx)
            for i in range(nt):
                ps, pb = cx.pool.ps()
                B.emit("pe", [MM(ps[:, 0:256], HT[:, ht, i * 128:(i + 1) * 128],
                                 (wa[:, ht, :] if ht < 11 else wb_[:, ht - 11, :]), ht == 0, ht == NHT - 1)
                              for ht in range(NHT)], reads=[b["HT"], wab, wbb], writes=[pb])
                e = ev_eng()
                fn = ACTF(Fb[:, i, cq * 256:(cq + 1) * 256], ps[:, 0:256], AF.Copy) if e == "act" else \
                    CP(Fb[:, i, cq * 256:(cq + 1) * 256], ps[:, 0:256])
                B.emit(e, fn, reads=[pb], **wr("F"))
                yield
        for i in range(nt):
            if sel_of_tile is not None:
                load_gg(sel_of_tile[i])
            tm_norm_residual(cx, i, Fb[:, i, :], [b["F"]], 1, EPS, True, out_rows=out_ap[i * 128:(i + 1) * 128, :])
            yield

    def run(g, tag=""):
        B.tag = tag
        for _ in g:
            pass

    def interleave(ga, gb, ra=1, rb=1):
        da = db = False
        while not (da and db):
            for _ in range(ra):
                if not da:
                    try:
                        next(ga)
                    except StopIteration:
                        da = True
            for _ in range(rb):
                if not db:
                    try:
                        next(gb)
                    except StopIteration:
                        db = True

    P_M1 = Pool_([6, 7])
    P_F = Pool_([0, 1, 2, 3])
    cxM1 = Cx(P_M1, UTa, ubA, 0, ybanks=[4, 5], wbs=[2, 3, 4])
    cxM2 = Cx(PALL, UTa, ubA, 12, wbs=[2, 3, 4])
    cxF = Cx(P_F, UTb, ubB, 4, wbs=[0, 1])
    cxFn = Cx(P_F, UTb, ubB, 8)

    cxPre = Cx(PALL, UTa, ubA, 0)

    def M1(src_rows, nt, segs, kind, with_y, cx=None):
        cx = cx or cxM1
        yield from prenorm_stream(cx, src_rows, nt, segs)
        yield from rnn_phase(cx, nt * 128, kind, with_y)

    def M2_prompt(bk, last_block):
        load_x(I["xm"][bk * 512:(bk + 1) * 512, :], 4)
        yield from qk_phase(cxM2, 512)
        w = load_w("w_kv", 0, cxM2)
        for i in range(4):
            last = (last_block and i == 3)
            kv_tile(cxM2, w, i * 128, 128, 1 + i, kout=(O["kp"], O["vp"]) if last else None)
        for p in range(4):
            tabs = [TAB0[:] if (bk == 0 and p == 0) else TAB[:, 0, :, :], TAB[:, 1, :, :]]
            yield from attention(cxM2, 128 * p, 128, [
                dict(kT=(lambda m, p=p: KT[:, m, 128 * p:128 * p + 128]), nk=128, va=VA[:, p, :, :], tab=tabs[0]),
                dict(kT=(lambda m, p=p: KT[:, m, 128 * p + 128:128 * p + 256]), nk=128, va=VA[:, p + 1, :, :], tab=tabs[1]),
            ])
        B.emit("act", ACTF(KT[:, :, 0:128], KT[:, :, 512:640], AF.Copy), reads=[b["KT"]], writes=[b["KT"]])
        B.emit("act", ACTF(VA[:, 0, :, 0:64], VA[:, 4, :, 0:64], AF.Copy), reads=[b["VA"]], writes=[b["VA"]])
        yield from merge_phase(cxM2, 512)
        if bk == 0:
            load_gg(0)
        yield from wout_phase(cxM2, 4, None)

    def F_pre(nt, segs):
        yield from prenorm_resident(cxF, nt, segs)

    def F_main(nt, sel, out_ap):
        cxFn.UT, cxFn.ub = cxF.UT, cxF.ub
        yield from ffn_phase_cx(nt, sel, out_ap)

    def ffn_phase_cx(nt, sel, out_ap):
        g = ffn_phase(cxF, nt, sel, out_ap)
        yield from g

    n_pre, n_main, do_sample = {2: (0, 1, False), 2.5: (4, 1, False), 3.1: (0, 4, False),
                                3.2: (0, 0, True)}.get(stage, (4, 4, True))
    PSEG = [(0, 512, 0)]
    SSEG = [(64 * s_, 64, 1 + s_) for s_ in range(4)]
    P_LO, P_HI = Pool_([0, 1, 2, 3]), Pool_([4, 5, 6, 7])

    def M2_sample(cx):
        load_x(I["xs"], 2)
        for s_ in range(4):
            load(KCT[:, s_, :, :], I["ck"][s_], [b["KCT"]], eng="pool")
            load(VAC[:, s_, :, 0:64], I["cv"][s_].rearrange("p (g d) -> p g d", g=4), [b["VAC"]], eng="pool")
            store(O["ks"][s_, 0:64, :], I["ckn"][s_, 64:128, :], [])
            store(O["vs"][s_, 0:64, :], I["cv"][s_, 64:128, :], [])
        yield from qk_phase(cx, 256)
        w = load_w("w_kv", 0, cx)
        for s_ in range(4):
            kv_tile(cx, w, 64 * s_, 64, 1 + s_, kout=(O["ks"][s_, 64:128, :], O["vs"][s_, 64:128, :]))
        for s_ in range(4):
            yield from attention(cx, 64 * s_, 64, [
                dict(kT=(lambda m, s_=s_: KCT[:, s_, m, :]), nk=128, va=VAC[:, s_, :, :], tab=TAB[:, 0, :, :]),
                dict(kT=(lambda m, s_=s_: KT[:, m, 128 + 64 * s_:128 + 64 * s_ + 64]), nk=64, va=VA[:, 1 + s_, :, :],
                     tab=TAB[:, 1, :, :]),
            ])
        yield from merge_phase(cx, 256)
        yield from wout_phase(cx, 2, [1, 2])

    def pre_block(pbk, cx):
        yield from M1(I["xp"][pbk * 512:(pbk + 1) * 512, :], 4, PSEG, "prompt", False, cx)
        if pbk == 3:
            cx2 = Cx(cx.pool, cx.UT, cx.ub, cx.ssc)
            yield from qk_phase(cx2, 512, k_only_cols=384)
            kv_tile(cx2, load_w("w_kv", 0, cx2), 384, 128, 0)

    def pre(pbk):
        run(pre_block(pbk, Cx(P_HI, UTb, ubB, 0, wbs=None)), f"pre{pbk}")
    if n_pre and do_sample:
        pre(0)
        pre(1)
        run(M1(I["xs"], 2, SSEG, "sample", True), "M1s")
        pre(2)
        run(M2_sample(Cx(P_LO, UTa, ubA, 12)), "M2s")
        pre(3)
        run(F_pre(2, SSEG), "Fs")
        run(F_main(2, [1, 2], O["y_s"]), "Fs")
        store(O["convs"], CONVS[:], [b["CONVS"]])
        store(O["hs"], HS[:], [b["HS"]])
    else:
        for pbk in range(n_pre):
            pre(pbk)
    if n_pre:
        B.emit("dve", TS_(HC[:], HC[:], FLAG[:, 0:1], None, ALU.mult), reads=[hb("HC", c_) for c_ in range(NCH)] + [b["FLAG"]], writes=[hb("HC", c_) for c_ in range(NCH)])
        B.emit("dve", TS_(CONVH[:].rearrange("p c j -> p (c j)"), CONVH[:].rearrange("p c j -> p (c j)"),
                          FLAG[:, 0:1], None, ALU.mult), reads=[hb("CONVH", c_) for c_ in range(NCH)] + [b["FLAG"]], writes=[hb("CONVH", c_) for c_ in range(NCH)])
    B.emit("pool", DMA(TAB0[:], I["tab0"]), writes=[b["TAB0"], b["GG"]], slot=cslot)
    for bk in range(n_main):
        run(M1(I["xm"][bk * 512:(bk + 1) * 512, :], 4, PSEG, "prompt", True), f"M1_{bk}")
        run(M2_prompt(bk, bk == n_main - 1), f"M2_{bk}")
        run(F_pre(4, PSEG), f"F_{bk}")
        run(F_main(4, None, I_out_m[bk]), f"F_{bk}")
    if do_sample and not n_pre:
        run(M1(I["xs"], 2, SSEG, "sample", True), "M1s")
        run(M2_sample(Cx(P_LO, UTa, ubA, 12)), "M2s")
        run(F_pre(2, SSEG), "Fs")
        run(F_main(2, [1, 2], O["y_s"]), "Fs")
        store(O["convs"], CONVS[:], [b["CONVS"]])
        store(O["hs"], HS[:], [b["HS"]])
    if n_main:
        store(O["convp"], CONVH[:], [hb("CONVH", c_) for c_ in range(NCH)])
        store(O["hp"], HC[:], [hb("HC", c_) for c_ in range(NCH)])
    B.barrier()
    return nc, B


def _relayout_w(w, kk=None):
    K, N = w.shape
    return np.ascontiguousarray(w.reshape(K // 128, 128, N).transpose(1, 0, 2))


def _chunks(w, cw):
    K, N = w.shape
    return np.ascontiguousarray(w.reshape(K // 128, 128, N // cw, cw).transpose(2, 1, 0, 3))


def _pp(v, n):
    return np.ascontiguousarray(v.reshape(n, 128).T)


def _tables():
    h = np.arange(1, 17, dtype=np.float32)
    slopes = np.exp2(-8.0 * h / 16).astype(np.float32)
    k = np.arange(128)
    q = np.arange(128)
    tab = np.zeros((128, 2, 16, 128), np.float32)
    for t in range(2):
        kchunk = 2 * t + k // 64 - 2
        kpos = kchunk * 64 + k % 64
        qchunk = q // 64
        qpos = q
        rel = kchunk[:, None] - qchunk[None, :]
        valid = (rel <= 0) & (rel >= -2)
        dist = np.abs(qpos[None, :] - kpos[:, None]).astype(np.float32)
        for hh in range(16):
            tab[:, t, hh, :] = np.where(valid, np.exp(-slopes[hh] * dist), 0.0)
    return tab


def prep_inputs(inp):
    f = lambda a: np.ascontiguousarray(np.asarray(a, dtype=np.float32))
    w_in = f(inp["w_in"][0])
    q_w, k_w, v_w = w_in[:, 0:1024], w_in[:, 1024:1280], w_in[:, 1280:1536]
    xr_w, yr_w = w_in[:, 1536:2816], w_in[:, 2816:4096]
    ga_w, gr_w = w_in[:, 4096:5120], w_in[:, 5120:6144]
    qcols = []
    for m in range(2):
        for j in range(4):
            for e in range(2):
                g = 2 * m + e
                qcols.append(np.arange(64) + g * 256 + j * 64)
    qcols = np.concatenate(qcols)
    xy = np.concatenate([np.concatenate([xr_w[:, c * 128:(c + 1) * 128], yr_w[:, c * 128:(c + 1) * 128]], 1)
                         for c in range(NCH)], 1)
    gg = np.concatenate([np.concatenate([ga_w[:, o * 128:(o + 1) * 128], gr_w[:, o * 128:(o + 1) * 128]], 1)
                         for o in range(8)], 1)
    gu = np.concatenate([np.concatenate([inp["w_ffn_gate"][0][:, h * 128:(h + 1) * 128],
                                         inp["w_ffn_up"][0][:, h * 128:(h + 1) * 128]], 1)
                         for h in range(NHT)], 1)
    tab = _tables()
    sel = np.zeros((5, 3, 128), np.float32)
    sel[0, 0, :] = 1.0
    for i in range(2):
        sel[1 + 2 * i, 1 + i, 0:64] = 1.0
        sel[2 + 2 * i, 1 + i, 64:128] = 1.0
    shared = {
        "w_ada": _relayout_w(f(inp["w_ada"][0])),
        "b_ada5": np.ascontiguousarray(np.broadcast_to(f(inp["b_ada"][0])[None, :], (5, 6144))),
        "w_q": _chunks(f(q_w[:, qcols]), 512), "w_k": _chunks(f(k_w), 256),
        "w_xy": _chunks(f(xy), 512), "w_kv": _chunks(f(np.concatenate([k_w, v_w], 1)), 512),
        "w_g": _chunks(f(gg), 512), "w_xr": _chunks(f(xr_w), 256),
        "w_ao": _chunks(f(inp["w_attn_o"][0]), 512), "w_ro": _chunks(f(inp["w_rnn_o"][0]), 256),
        "w_out": _chunks(f(inp["w_out"][0]), 512), "w_gu": _chunks(f(gu), 512),
        "w_dn": np.ascontiguousarray(_chunks(f(inp["w_ffn_down"][0]), 256).reshape(4, 128, 2, 11, 256).transpose(0, 2, 1, 3, 4).reshape(8, 128, 11, 256)),
        "w_rga": np.ascontiguousarray(f(inp["w_rg_a"][0]).transpose(1, 0, 2)),
        "w_rgx": np.ascontiguousarray(f(inp["w_rg_x"][0]).transpose(1, 0, 2)),
        "gpre": np.ascontiguousarray(np.stack([_pp(f(inp["g_pre_mix"][0]), 8),
                                               _pp(f(inp["g_pre_ffn"][0]), 8)], 1)),
        "gpost5": np.ascontiguousarray(np.broadcast_to(
            np.stack([f(inp["g_post_mix"][0]), f(inp["g_post_ffn"][0])], 0)[None], (5, 2, 1024))),
        "sinks": np.ascontiguousarray(np.broadcast_to(f(inp["attn_sinks"][0])[None, :], (128, 16))),
        "wconv": np.ascontiguousarray(f(inp["w_conv"][0]).reshape(4, NCH, 128).transpose(2, 1, 0)),
        "bconv": _pp(f(inp["b_conv"][0]), NCH), "ba": _pp(f(inp["b_rg_a"][0]), NCH),
        "bx": _pp(f(inp["b_rg_x"][0]), NCH), "lam": _pp(f(inp["rg_lambda"][0]), NCH),
        "tab": tab, "ident": np.eye(128, dtype=np.float32), "sel": sel,
    }
    xp_, xs_ = f(inp["x_prompt"]), f(inp["x_sample"])
    cp_, cs_ = f(inp["c_prompt"]), f(inp["c_sample"])
    ck_, cv_ = f(inp["cache_k"][0]), f(inp["cache_v"][0])
    sc_, sh_ = f(inp["state_conv"][0]), f(inp["state_h"][0])
    maps = []
    for c in range(NCORES):
        s, half = c // 2, c % 2
        cc = np.concatenate([cp_[s:s + 1], cs_[4 * c:4 * c + 4]], 0)
        cT = np.ascontiguousarray(cc.reshape(5, 8, 128).transpose(2, 1, 0))
        ck = ck_[4 * c:4 * c + 4]
        ckT = np.ascontiguousarray(ck.reshape(4, 128, 2, 2, 64).transpose(0, 3, 4, 2, 1).reshape(4, 128, 2, 128))
        cv = np.ascontiguousarray(cv_[4 * c:4 * c + 4].reshape(4, 128, 256))
        sconv = np.ascontiguousarray(sc_[4 * c:4 * c + 4].reshape(4, 3, NCH, 128).transpose(3, 2, 0, 1))
        sh0 = np.ascontiguousarray(sh_[4 * c:4 * c + 4].reshape(4, NCH, 128).transpose(2, 1, 0))
        tab0 = tab[:, 0].copy() if half == 1 else np.zeros((128, 16, 128), np.float32)
        m = dict(shared)
        m.update({
            "xm": np.ascontiguousarray(xp_[s, half * TP:(half + 1) * TP]),
            "xp": np.ascontiguousarray(xp_[s, 0:TP]),
            "xs": np.ascontiguousarray(xs_[4 * c:4 * c + 4].reshape(TS, D)),
            "cT": cT, "flag": np.full((128, 1), float(half), np.float32),
            "ck": ckT, "cv": cv, "ckn": np.ascontiguousarray(ck.reshape(4, 128, 256)), "sconv": sconv, "sh0": sh0, "tab0": tab0,
        })
        maps.append(m)
    return maps


def kernel(**inputs):
    maps = prep_inputs(inputs)
    nc, _ = build_nc()
    res = run_bass_kernel_spmd(nc, maps, core_ids=list(range(NCORES)))
    R = res.results
    y_p = np.zeros((4, 4096, D), np.float32)
    y_s = np.zeros((32, 64, D), np.float32)
    k_p = np.zeros((1, 4, 128, 4, 64), np.float32)
    v_p = np.zeros_like(k_p)
    c_p = np.zeros((1, 4, 3, 1280), np.float32)
    h_p = np.zeros((1, 4, 1280), np.float32)
    k_s = np.zeros((1, 32, 128, 4, 64), np.float32)
    v_s = np.zeros_like(k_s)
    c_s = np.zeros((1, 32, 3, 1280), np.float32)
    h_s = np.zeros((1, 32, 1280), np.float32)
    for c in range(NCORES):
        s, half = c // 2, c % 2
        r = R[c]
        y_p[s, half * TP:(half + 1) * TP] = r["y_m"]
        y_s[4 * c:4 * c + 4] = r["y_s"].reshape(4, 64, D)
        if half == 1:
            k_p[0, s] = r["kp"].reshape(128, 4, 64)
            v_p[0, s] = r["vp"].reshape(128, 4, 64)
            c_p[0, s] = r["convp"].transpose(2, 1, 0).reshape(3, 1280)
            h_p[0, s] = r["hp"].T.reshape(1280)
        k_s[0, 4 * c:4 * c + 4] = r["ks"].reshape(4, 128, 4, 64)
        v_s[0, 4 * c:4 * c + 4] = r["vs"].reshape(4, 128, 4, 64)
        c_s[0, 4 * c:4 * c + 4] = r["convs"].transpose(2, 3, 1, 0).reshape(4, 3, 1280)
        h_s[0, 4 * c:4 * c + 4] = r["hs"].transpose(2, 1, 0).reshape(4, 1280)
    return (y_p, y_s, k_p, v_p, c_p, h_p, k_s, v_s, c_s, h_s)
```
